# Optimizing a Trainium2 kernel written in Bass

```python
import math
import jax, jax.numpy as jnp
from jax import lax
import numpy as np

D_MODEL = 2048
BATCH = 8
SEQ = 2048
DEPTH = 1

PLE_DIM = 256
D_FF = 5632
DIFF_HEADS = 8
DIFF_HEAD_DIM = 64
DIFF_V_DIM = 2 * DIFF_HEAD_DIM
DIFF_WIDTH = DIFF_HEADS * 2 * DIFF_HEAD_DIM
HGRN_HEADS = 8
HGRN_K = 128
HGRN_V = 128
HGRN_WIDTH = HGRN_HEADS * HGRN_K
HGRN_CHUNK = 64
Q_BLOCK = 128
REL_BUCKETS = 32
REL_MAX_DIST = 128
N_IN = 3 * DIFF_WIDTH + 4 * HGRN_WIDTH + 2 * D_MODEL
EPS = 1e-6

kernel_name = "hybrid_diffattn_hgrn2_macaron_block"


def rmsnorm(x, g):
    xf = x.astype(jnp.float32)
    y = xf * lax.rsqrt(jnp.mean(xf * xf, axis=-1, keepdims=True) + EPS)
    return (y * g.astype(jnp.float32)).astype(x.dtype)


def swiglu(h, w_gate, w_up, w_down):
    return (jax.nn.silu(h @ w_gate) * (h @ w_up)) @ w_down


def t5_bucket(dist):
    n = jnp.maximum(dist, 0)
    max_exact = REL_BUCKETS // 2
    nf = jnp.maximum(n, 1).astype(jnp.float32)
    large = max_exact + (jnp.log(nf / max_exact) / math.log(REL_MAX_DIST / max_exact)
                         * (REL_BUCKETS - max_exact)).astype(jnp.int32)
    large = jnp.minimum(large, REL_BUCKETS - 1)
    return jnp.where(n < max_exact, n, large)


def diff_attention(q, k, v, q_gain, k_gain, lam, rel_bias, subln_gain, lambda_init):
    B, S, _ = q.shape
    q = rmsnorm(q.reshape(B, S, DIFF_HEADS, 2, DIFF_HEAD_DIM), q_gain).transpose(0, 2, 3, 1, 4)
    k = rmsnorm(k.reshape(B, S, DIFF_HEADS, 2, DIFF_HEAD_DIM), k_gain).transpose(0, 2, 3, 1, 4)
    v = v.reshape(B, S, DIFF_HEADS, DIFF_V_DIM).transpose(0, 2, 1, 3).astype(jnp.float32)
    q1, q2 = q[:, :, 0], q[:, :, 1]
    k1, k2 = k[:, :, 0], k[:, :, 1]
    scale = DIFF_HEAD_DIM ** -0.5
    k_pos = jnp.arange(S)

    def block(blk):
        start = blk * Q_BLOCK
        q1b = lax.dynamic_slice_in_dim(q1, start, Q_BLOCK, axis=2)
        q2b = lax.dynamic_slice_in_dim(q2, start, Q_BLOCK, axis=2)
        dist = (start + jnp.arange(Q_BLOCK))[:, None] - k_pos[None, :]
        bias = jnp.transpose(rel_bias[t5_bucket(dist)], (2, 0, 1)).astype(jnp.float32)
        visible = dist >= 0

        def probs(qb, kk):
            s = jnp.einsum('bhqd,bhkd->bhqk', qb, kk).astype(jnp.float32) * scale + bias
            return jax.nn.softmax(jnp.where(visible, s, -jnp.inf), axis=-1)

        a = probs(q1b, k1) - lam * probs(q2b, k2)
        return jnp.einsum('bhqk,bhkv->bhqv', a, v)

    o = lax.map(block, jnp.arange(S // Q_BLOCK))
    o = o.transpose(1, 0, 3, 2, 4).reshape(B, S, DIFF_HEADS, DIFF_V_DIM)
    o = rmsnorm(o, subln_gain) * (1.0 - lambda_init)
    return o.reshape(B, S, DIFF_WIDTH).astype(q.dtype)


def hgrn2(q, f_pre, i_in, og, lb, norm_gain):
    B, S, _ = q.shape
    nc = S // HGRN_CHUNK
    f32 = jnp.float32

    def chunks(t, dh):
        return t.reshape(B, nc, HGRN_CHUNK, HGRN_HEADS, dh).transpose(1, 0, 3, 2, 4)

    lb = lb.astype(f32)
    forget = lb + (1.0 - lb) * jax.nn.sigmoid(f_pre.astype(f32))
    qf = jax.nn.silu(q.astype(f32))
    kf = 1.0 - forget
    logf = jnp.log(forget)
    tri = jnp.arange(HGRN_CHUNK)[:, None] >= jnp.arange(HGRN_CHUNK)[None, :]

    def step(state, inp):
        qc, kc, vc, gc = inp
        b = jnp.cumsum(gc, axis=2)
        rel = jnp.where(tri[:, :, None], b[:, :, :, None, :] - b[:, :, None, :, :], -jnp.inf)
        scores = jnp.einsum('bhtk,bhsk,bhtsk->bhts', qc, kc, jnp.exp(rel))
        o = (jnp.einsum('bhts,bhsv->bhtv', scores, vc)
             + jnp.einsum('bhtk,bhkv->bhtv', qc * jnp.exp(b), state))
        b_last = b[:, :, -1:, :]
        state = (jnp.exp(b_last[:, :, 0, :])[..., None] * state
                 + jnp.einsum('bhsk,bhsv->bhkv', kc * jnp.exp(b_last - b), vc))
        return state, o

    state0 = jnp.zeros((B, HGRN_HEADS, HGRN_K, HGRN_V), f32)
    _, o = lax.scan(step, state0, (chunks(qf, HGRN_K), chunks(kf, HGRN_K),
                                   chunks(i_in.astype(f32), HGRN_V), chunks(logf, HGRN_K)))
    o = o.transpose(1, 0, 3, 2, 4).reshape(B, S, HGRN_HEADS, HGRN_V)
    o = rmsnorm(o, norm_gain) * jax.nn.silu(og.astype(f32)).reshape(B, S, HGRN_HEADS, HGRN_V)
    return o.reshape(B, S, HGRN_WIDTH).astype(q.dtype)


def setup_inputs(seed: int = 0) -> dict:
    key = jax.random.key(seed)
    ks = jax.random.split(key, 32)

    def nrm(k, shape, scale):
        return jax.random.normal(k, shape, jnp.float32) * scale

    def gain(k, shape):
        return 1.0 + 0.1 * jax.random.normal(k, shape, jnp.float32)

    L = DEPTH
    return {
        'x': nrm(ks[0], (BATCH, SEQ, D_MODEL), 1.0),
        'p': nrm(ks[1], (L, BATCH, SEQ, PLE_DIM), 1.0),
        'ffn1_norm': gain(ks[2], (L, D_MODEL)),
        'ffn1_w_gate': nrm(ks[3], (L, D_MODEL, D_FF), D_MODEL ** -0.5),
        'ffn1_w_up': nrm(ks[4], (L, D_MODEL, D_FF), D_MODEL ** -0.5),
        'ffn1_w_down': nrm(ks[5], (L, D_FF, D_MODEL), D_FF ** -0.5),
        'mix_norm': gain(ks[6], (L, D_MODEL)),
        'w_in': nrm(ks[7], (L, D_MODEL, N_IN), D_MODEL ** -0.5),
        'q_norm': gain(ks[8], (L, DIFF_HEAD_DIM)),
        'k_norm': gain(ks[9], (L, DIFF_HEAD_DIM)),
        'lambda_q1': nrm(ks[10], (L, DIFF_HEAD_DIM), 0.1),
        'lambda_k1': nrm(ks[11], (L, DIFF_HEAD_DIM), 0.1),
        'lambda_q2': nrm(ks[12], (L, DIFF_HEAD_DIM), 0.1),
        'lambda_k2': nrm(ks[13], (L, DIFF_HEAD_DIM), 0.1),
        'diff_subln': gain(ks[14], (L, DIFF_V_DIM)),
        'rel_bias': nrm(ks[15], (REL_BUCKETS, DIFF_HEADS), 0.5),
        'hgrn_lb_logits': nrm(ks[16], (L + 1, HGRN_WIDTH), 0.5),
        'hgrn_norm': gain(ks[17], (L, HGRN_V)),
        'w_branch_a': nrm(ks[18], (L, DIFF_WIDTH, D_MODEL), DIFF_WIDTH ** -0.5),
        'w_branch_b': nrm(ks[19], (L, HGRN_WIDTH, D_MODEL), HGRN_WIDTH ** -0.5),
        'w_out': nrm(ks[20], (L, D_MODEL, D_MODEL), D_MODEL ** -0.5),
        'ffn2_norm': gain(ks[21], (L, D_MODEL)),
        'ffn2_w_gate': nrm(ks[22], (L, D_MODEL, D_FF), D_MODEL ** -0.5),
        'ffn2_w_up': nrm(ks[23], (L, D_MODEL, D_FF), D_MODEL ** -0.5),
        'ffn2_w_down': nrm(ks[24], (L, D_FF, D_MODEL), D_FF ** -0.5),
        'ple_gate_norm': gain(ks[25], (L, D_MODEL)),
        'w_ple_gate': nrm(ks[26], (L, D_MODEL, D_MODEL), D_MODEL ** -0.5),
        'w_ple_proj': nrm(ks[27], (L, PLE_DIM, D_MODEL), PLE_DIM ** -0.5),
        'ple_post_norm': gain(ks[28], (L, D_MODEL)),
    }


def reference(x, p, ffn1_norm, ffn1_w_gate, ffn1_w_up, ffn1_w_down, mix_norm, w_in,
              q_norm, k_norm, lambda_q1, lambda_k1, lambda_q2, lambda_k2, diff_subln,
              rel_bias, hgrn_lb_logits, hgrn_norm, w_branch_a, w_branch_b, w_out,
              ffn2_norm, ffn2_w_gate, ffn2_w_up, ffn2_w_down,
              ple_gate_norm, w_ple_gate, w_ple_proj, ple_post_norm):
    lower_bounds = jnp.cumsum(jax.nn.softmax(hgrn_lb_logits.astype(jnp.float32), axis=0), axis=0)
    split_points = list(np.cumsum([DIFF_WIDTH, DIFF_WIDTH, DIFF_WIDTH,
                                   HGRN_WIDTH, HGRN_WIDTH, HGRN_WIDTH, HGRN_WIDTH, D_MODEL]))
    for i in range(DEPTH):
        x = x + 0.5 * swiglu(rmsnorm(x, ffn1_norm[i]), ffn1_w_gate[i], ffn1_w_up[i], ffn1_w_down[i])

        h = rmsnorm(x, mix_norm[i])
        proj = h @ w_in[i]
        dq, dk, dv, hq, hf, hi, hg, gate_a, gate_b = jnp.split(proj, split_points, axis=-1)

        lambda_init = 0.8 - 0.6 * math.exp(-0.3 * i)
        lam = (jnp.exp(jnp.sum(lambda_q1[i].astype(jnp.float32) * lambda_k1[i].astype(jnp.float32)))
               - jnp.exp(jnp.sum(lambda_q2[i].astype(jnp.float32) * lambda_k2[i].astype(jnp.float32)))
               + lambda_init)
        y_a = diff_attention(dq, dk, dv, q_norm[i], k_norm[i], lam, rel_bias, diff_subln[i], lambda_init)
        y_b = hgrn2(hq, hf, hi, hg, lower_bounds[i], hgrn_norm[i])

        merged = jax.nn.sigmoid(gate_a) * (y_a @ w_branch_a[i]) + jax.nn.sigmoid(gate_b) * (y_b @ w_branch_b[i])
        x = x + merged @ w_out[i]

        x = x + 0.5 * swiglu(rmsnorm(x, ffn2_norm[i]), ffn2_w_gate[i], ffn2_w_up[i], ffn2_w_down[i])

        ple = rmsnorm(p[i] @ w_ple_proj[i], ple_post_norm[i])
        x = x + jax.nn.sigmoid(rmsnorm(x, ple_gate_norm[i]) @ w_ple_gate[i]) * ple
    return x
```

```python
import os
import math
from contextlib import ExitStack, contextmanager
import numpy as np
import ml_dtypes
import concourse.bass as bass
import concourse.mybir as mybir
from concourse.bass_utils import run_bass_kernel_spmd

F32 = mybir.dt.float32
BF16 = mybir.dt.bfloat16
AF = mybir.ActivationFunctionType
ALU = mybir.AluOpType
AX = mybir.AxisListType

S = 2048
D = 2048
DFF = 5632
NIN = 11264
EPS = 1e-6
NCORES = 8
LAMBDA_INIT = 0.8 - 0.6 * math.exp(-0.3 * 0)
BVL = 1152
SUB = int(os.environ.get("MK_SUB", "9"))
SUBA = int(os.environ.get("MK_SUBA", "9"))
NH = int(os.environ.get("MK_NH", "8"))


class Res:
    __slots__ = ("w", "r", "isbank")

    def __init__(self, K, isbank=False):
        self.w = dict(K.grave)
        self.r = {}
        self.isbank = isbank


class Scope:
    def __init__(self, K):
        self.K = K
        self.es = ExitStack()
        self.rs = []

    def sb(self, name, shape, dt):
        self.K.uid += 1
        return self.es.enter_context(self.K.nc.sbuf_tensor(f"{name}_{self.K.uid}", shape, dt))

    def res(self):
        r = Res(self.K)
        self.rs.append(r)
        return r

    def sbr(self, name, shape, dt):
        return self.sb(name, shape, dt), self.res()


class Sched:
    def __init__(self, nc, es):
        self.nc = nc
        self.es = es
        self.engs = dict(pe=nc.tensor, act=nc.scalar, dve=nc.vector, pool=nc.gpsimd, sp=nc.sync)
        self.sem = {}
        self.cnt = {}
        self.prog = {}
        self.waited = {}
        self.grave = {}
        self.uid = 0
        for e in self.engs:
            self.sem["s_" + e] = es.enter_context(nc.semaphore("s_" + e))
            self.cnt["s_" + e] = 0
            self.prog[e] = []
            self.waited[e] = {}
        self.nphys = {"pool": 26, "sp": 64}
        self.k2p = {"pool": {}, "sp": {}}
        for q, n in self.nphys.items():
            for i in range(n):
                self.sem[f"{q}{i}"] = es.enter_context(nc.semaphore(f"{q}{i}"))
                self.cnt[f"{q}{i}"] = 0
        self.banks = []
        self.ps = es.enter_context(nc.psum_tensor("psall", [128, 8, 512], F32))
        for i in range(8):
            self.banks.append((self.ps[:, i, :], Res(self, True)))
        self.bi = 0

    def bank(self):
        b = self.banks[self.bi]
        self.bi = (self.bi + 1) % 8
        return b

    @contextmanager
    def scope(self):
        sc = Scope(self)
        try:
            yield sc
        finally:
            for r in sc.rs:
                for d in (r.w, r.r):
                    for k, v in d.items():
                        if self.grave.get(k, 0) < v:
                            self.grave[k] = v
            sc.es.close()

    def _need(self, e, reads, writes, merge):
        need = {}
        for r in reads:
            for k, v in r.w.items():
                if need.get(k, 0) < v:
                    need[k] = v
        for w in writes:
            for d in (w.w, w.r):
                for k, v in d.items():
                    if need.get(k, 0) < v:
                        need[k] = v
        for w in merge:
            for k, v in w.r.items():
                if need.get(k, 0) < v:
                    need[k] = v
        if e == "pe":
            need.pop("s_pe", None)
        wd = self.waited[e]
        waits = []
        for k, v in need.items():
            if wd.get(k, 0) < v:
                wd[k] = v
                waits.append((k, v))
        return waits

    def _mark(self, tok, reads, writes, merge):
        k, v = tok
        for r in reads:
            r.r[k] = v
        for w in writes:
            w.w = {k: v}
            w.r = {}
        for w in merge:
            w.w[k] = v

    def op(self, e, fn, reads=(), writes=(), merge=()):
        if e != "pe":
            br = [r for r in reads if r.isbank]
            if br:
                reads = [r for r in reads if not r.isbank]
                writes = list(writes) + br
        waits = self._need(e, reads, writes, merge)
        key = "s_" + e
        self.cnt[key] += 1
        tok = (key, self.cnt[key])
        self.prog[e].append((waits, fn, tok, 1))
        self._mark(tok, reads, writes, merge)

    def dma(self, q, key, out, in_, reads=(), writes=(), merge=(), slow=False):
        k2p = self.k2p[q]
        if key not in k2p:
            k2p[key] = f"{q}{len(k2p) % self.nphys[q]}"
        key = k2p[key]
        waits = self._need(q, reads, writes, merge)
        c = self.cnt[key]
        if c and self.waited[q].get(key, 0) < c:
            self.waited[q][key] = c
            waits.append((key, c))
        self.cnt[key] += 16
        tok = (key, self.cnt[key])
        if slow:
            fn = lambda eng: eng.dma_start(out=out, in_=in_, allow_slow_non_contiguous=True)
        else:
            fn = lambda eng: eng.dma_start(out=out, in_=in_)
        self.prog[q].append((waits, fn, tok, 16))
        self._mark(tok, reads, writes, merge)

    def wait_all(self, e, rs):
        waits = self._need(e, rs, (), ())
        self.prog[e].append((waits, None, None, 0))

    def emit(self):
        nc = self.nc
        with nc.Block() as block:
            for e, deco in (("sp", block.sync), ("act", block.scalar), ("dve", block.vector),
                            ("pool", block.gpsimd), ("pe", block.tensor)):
                prog = self.prog[e]

                def body(eng, prog=prog):
                    for waits, fn, tok, inc in prog:
                        for k, v in waits:
                            eng.wait_ge(self.sem[k], v)
                        if fn is not None:
                            fn(eng).then_inc(self.sem[tok[0]], inc)

                deco(body)


def I_act(out, in_, func, bias=None, scale=None, accum=None):
    def f(eng):
        kw = {}
        if bias is not None:
            kw["bias"] = bias
        if scale is not None:
            kw["scale"] = scale
        if accum is not None:
            kw["accum_out"] = accum
        return eng.activation(out=out, in_=in_, func=func, **kw)
    return f


def I_stt(out, in0, scalar, in1, op0, op1):
    return lambda eng: eng.scalar_tensor_tensor(out=out, in0=in0, scalar=scalar, in1=in1, op0=op0, op1=op1)


def I_ts(out, in0, s1, s2, op0, op1=None):
    if op1 is None:
        return lambda eng: eng.tensor_scalar(out=out, in0=in0, scalar1=s1, scalar2=None, op0=op0)
    return lambda eng: eng.tensor_scalar(out=out, in0=in0, scalar1=s1, scalar2=s2, op0=op0, op1=op1)


def I_tt(out, in0, in1, op):
    return lambda eng: eng.tensor_tensor(out=out, in0=in0, in1=in1, op=op)


def I_copy(out, in_):
    return lambda eng: eng.tensor_copy(out=out, in_=in_)


def I_acopy(out, in_):
    return lambda eng: eng.activation(out=out, in_=in_, func=AF.Copy)


def I_mm(lst):
    def f(eng):
        ins = None
        for (o, l, r, st, sp) in lst:
            ins = eng.matmul(o, lhsT=l, rhs=r, start=st, stop=sp)
        return ins
    return f


def I_tr(lst, ident):
    def f(eng):
        ins = None
        for (o, i) in lst:
            ins = eng.transpose(out=o, in_=i, identity=ident)
        return ins
    return f


def I_memset(ap, v):
    return lambda eng: eng.memset(ap, v)


def rsqrt_act(K, out, in_, add, reads, wres):
    K.op("act", I_act(out, in_, AF.Ln, bias=add), reads=reads, writes=[wres])
    K.op("act", I_act(out, out, AF.Exp, scale=-0.5), reads=[wres], writes=[wres])


def copy_on(K, e, out, in_, reads, writes, merge=()):
    K.op(e, I_acopy(out, in_) if e == "act" else I_copy(out, in_), reads=reads, writes=writes, merge=merge)


class WRing:
    def __init__(self, K, sc, n, nm):
        self.slots = []
        for i in range(n):
            t, r = sc.sbr(f"{nm}w{i}", [128, 16, 128], BF16)
            self.slots.append((t, r, f"{nm}w{i}"))
        self.i = 0
        self.n = n

    def next(self):
        s = self.slots[self.i]
        self.i = (self.i + 1) % self.n
        return s


def wview(W):
    return W.rearrange("(kc p) n -> p kc n", p=128)


def gemm_fm_gen(K, wr, groups, epilogue, bankfn=None, split=1, prime=False):
    bankfn = bankfn or K.bank
    ppg = max(len(g) for g in groups)
    ahead = max(1, wr.n // ppg - 1)
    pending = {}

    def issue(g):
        sl = []
        for (W, c0, KC, src, sres) in groups[g]:
            t, r, key = wr.next()
            K.dma("pool", key, t[:, 0:KC, :], wview(W)[:, :, c0:c0 + 128], writes=[r])
            sl.append((t, r))
        pending[g] = sl

    for g in range(min(ahead, len(groups))):
        issue(g)
    if prime:
        yield
    for g in range(len(groups)):
        if g + ahead < len(groups):
            issue(g + ahead)
        sl = pending.pop(g)
        for t4 in range(4):
            ps = []
            for (W, c0, KC, src, sres), (wt, wres) in zip(groups[g], sl):
                bank, bres = bankfn()
                mm = [(bank[:], wt[:, kc, :], src[:, kc, t4 * 512:(t4 + 1) * 512], kc == 0, kc == KC - 1)
                      for kc in range(KC)]
                step = (KC + split - 1) // split
                for si, k0 in enumerate(range(0, KC, step)):
                    K.op("pe", I_mm(mm[k0:k0 + step]), reads=[wres] + sres(t4),
                         writes=[bres] if si == 0 else [], merge=[] if si == 0 else [bres])
                    if split > 1:
                        yield
                ps.append((bank, bres))
            epilogue(g, t4, ps)
            yield


def gemm_fm(K, wr, groups, epilogue):
    for _ in gemm_fm_gen(K, wr, groups, epilogue):
        pass


def interleave(main, side, per):
    for _ in main:
        for _k in range(per):
            if side is not None:
                try:
                    next(side)
                except StopIteration:
                    side = None
    if side is not None:
        for _ in side:
            pass


def build(stage=99, debug=False):
    nc = bass.Bass("TRN2", target_bir_lowering=False)
    es = ExitStack()

    def din(name, shape, dt=F32):
        return nc.dram_tensor(name, shape, dt, kind="ExternalInput").ap()

    def dscr(name, shape, dt, dbg=False):
        if dbg and debug:
            return nc.dram_tensor(name, shape, dt, kind="ExternalOutput").ap()
        return nc.dram_tensor(name, shape, dt).ap()

    x_in = din("x", [S, D])
    p_in = din("p", [S, 256])
    out = nc.dram_tensor("out", [S, D], F32, kind="ExternalOutput").ap()
    W = {}
    for nm, shp in (("ffn1_w_gate", [D, DFF]), ("ffn1_w_up", [D, DFF]), ("ffn1_w_down", [DFF, D]),
                    ("w_in", [D, NIN]), ("w_branch_a", [1024, D]), ("w_branch_b", [1024, D]),
                    ("w_out", [D, D]), ("ffn2_w_gate", [D, DFF]), ("ffn2_w_up", [D, DFF]),
                    ("ffn2_w_down", [DFF, D]), ("w_ple_gate", [D, D]), ("w_ple_proj", [256, D])):
        W[nm] = din(nm, shp)
    V = {}
    for nm, n in (("ffn1_norm", D), ("mix_norm", D), ("ffn2_norm", D), ("ple_gate_norm", D),
                  ("ple_post_norm", D), ("q_norm", 64), ("k_norm", 64), ("lambda_q1", 64),
                  ("lambda_k1", 64), ("lambda_q2", 64), ("lambda_k2", 64), ("diff_subln", 128),
                  ("hgrn_norm", 128)):
        V[nm] = din(nm, [1, n])
    rel_bias = din("rel_bias", [32, 8])
    lb_logits = din("hgrn_lb_logits", [2, 1024])
    c_ident = din("c_ident", [128, 128], BF16)
    c_blk64 = din("c_blk64", [128, 128], BF16)
    c_mask2 = din("c_mask2", [128, 128], F32)
    c_scan = din("c_scan", [1, S], F32)
    c_oh = din("c_oh", [33, BVL], F32)

    x1 = dscr("x1", [S, D], F32, dbg=True) if stage > 1 else out
    x2 = dscr("x2", [S, D], F32, dbg=True) if stage > 2 else out
    x3 = dscr("x3", [S, D], F32, dbg=True) if stage > 3 else out
    gT = dscr("gT", [16, 128, 44, 128], BF16)
    mT = dscr("mT", [16, 128, 16, 128], BF16)
    yaT = dscr("yaT", [8, 128, S], BF16, dbg=True)
    ybT = dscr("ybT", [8, 128, S], BF16, dbg=True)
    BV = dscr("BV", [8, 128, BVL], F32)
    ple_d = dscr("ple", [S, D], F32)

    K = Sched(nc, es)
    top = Scope(K)
    es.enter_context(top.es)

    ident, ident_r = top.sbr("ident", [128, 128], BF16)
    K.dma("sp", "c_id", ident[:], c_ident, writes=[ident_r])

    with K.scope() as s0:
        rbx, rbx_r = s0.sbr("rbx", [33, 8], F32)
        K.op("dve", I_memset(rbx[:], -30000.0), writes=[rbx_r])
        K.dma("sp", "rbx", rbx[0:32, :], rel_bias, writes=[rbx_r])
        BV_res = [top.res() for _ in range(8)]
        with K.scope() as sbv:
            oh, oh_r = sbv.sbr("oh", [33, BVL], F32)
            K.dma("sp", "oh", oh[:], c_oh, writes=[oh_r])
            rbhs = [sbv.sbr("rbh", [33, 128], F32) for _ in range(2)]
            bvss = [sbv.sbr("bvs", [128, BVL], F32) for _ in range(2)]
            for h in range(8):
                rbh, rbh_r = rbhs[h % 2]
                bvs, bvs_r = bvss[h % 2]
                K.op("dve", I_copy(rbh[:], rbx[:, h:h + 1].to_broadcast([33, 128])), reads=[rbx_r],
                     writes=[rbh_r])
                for c3 in range(3):
                    bank, bres = K.bank()
                    K.op("pe", I_mm([(bank[:, 0:384], rbh[:], oh[:, c3 * 384:(c3 + 1) * 384], True, True)]),
                         reads=[rbh_r, oh_r], writes=[bres])
                    copy_on(K, "dve", bvs[:, c3 * 384:(c3 + 1) * 384], bank[:, 0:384], [bres],
                            [bvs_r] if c3 == 0 else [], merge=[] if c3 == 0 else [bvs_r])
                K.dma("sp", f"bvst{h % 2}", BV[h], bvs[:], reads=[bvs_r], writes=[BV_res[h]])


    def tok_rows(ap, tt):
        return ap[tt * 128:(tt + 1) * 128, :]

    def norm_T(src, src_res, gain, hT, hT_res, nm):
        with K.scope() as sc:
            gb, gb_r = sc.sbr("gb", [128, D], F32)
            xin = [sc.sbr("xin", [128, D], F32) for _ in range(3)]
            hn = [sc.sbr("hn", [128, D], BF16) for _ in range(3)]
            junk, junk_r = sc.sbr("junk", [128, D], BF16)
            ss = [sc.sbr("ss", [128, 1], F32) for _ in range(3)]
            rs = [sc.sbr("rs", [128, 1], F32) for _ in range(3)]
            K.dma("sp", nm + "gb", gb[:], gain[0].partition_broadcast(128), writes=[gb_r])
            K.op("dve", I_ts(gb[:], gb[:], math.sqrt(D), None, ALU.mult), reads=[gb_r], writes=[gb_r])
            def stageA(tt):
                b = tt % 3
                xt, xr = xin[b]
                K.dma("sp", f"{nm}xin{b}", xt[:], tok_rows(src, tt),
                      reads=[src_res[tt]] if src_res else [], writes=[xr])
                K.op("act", I_act(junk[:], xt[:], AF.Square, accum=ss[b][0][:]), reads=[xr],
                     writes=[junk_r, ss[b][1]])
                rsqrt_act(K, rs[b][0][:], ss[b][0][:], D * EPS, [ss[b][1]], rs[b][1])
                K.op("dve", I_stt(hn[b][0][:], xt[:], rs[b][0][:], gb[:], ALU.mult, ALU.mult),
                     reads=[xr, rs[b][1], gb_r], writes=[hn[b][1]])

            stageA(0)
            for tt in range(16):
                b = tt % 3
                if tt + 1 < 16:
                    stageA(tt + 1)
                for half in range(2):
                    bank, bres = K.bank()
                    pb = bank[:].bitcast(BF16)
                    K.op("pe", I_tr([(pb[:, j * 128:(j + 1) * 128],
                                      hn[b][0][:, (half * 8 + j) * 128:(half * 8 + j + 1) * 128])
                                     for j in range(8)], ident[:]),
                         reads=[hn[b][1], ident_r], writes=[bres])
                    copy_on(K, "act" if half == 0 else "dve",
                            hT[:, half * 8:(half + 1) * 8, tt * 128:(tt + 1) * 128],
                            pb.rearrange("p (j c) -> p j c", j=8), [bres], [], merge=[hT_res[tt]])

    def tm_loadw(Wap, KC, nm, t, rl, fb):
        kch = [(k0, min(KC, k0 + 11)) for k0 in range(0, KC, 11)]
        for ci, (k0, k1) in enumerate(kch):
            K.dma("pool", f"{nm}wd{fb % 2}_{ci}", t[:, k0:k1, :],
                  wview(Wap)[:, k0:k1, fb * 512:(fb + 1) * 512], writes=[rl[ci]])

    def gemm_tm(Wap, KC, lhs_get, pre, epilogue, nm, nfb=4, wd0=None):
        with K.scope() as sc:
            kch = [(k0, min(KC, k0 + 11)) for k0 in range(0, KC, 11)]
            wd = [wd0] if wd0 is not None else []
            while len(wd) < 2:
                t = sc.sb("wd", [128, KC, 512], BF16)
                wd.append((t, [sc.res() for _ in kch]))

            def loadw(fb):
                t, rl = wd[fb % 2]
                tm_loadw(Wap, KC, nm, t, rl, fb)

            seq = [(fb, tt) for fb in range(nfb) for tt in range(16)]
            if wd0 is None:
                loadw(0)
            for i in range(min(2, len(seq))):
                pre(sc, i, *seq[i])
            for i, (fb, tt) in enumerate(seq):
                if tt == 0 and fb + 1 < nfb:
                    loadw(fb + 1)
                if i + 2 < len(seq):
                    pre(sc, i + 2, *seq[i + 2])
                lt, lres = lhs_get(i, fb, tt)
                wt, wrl = wd[fb % 2]
                bank, bres = K.bank()
                mm = [(bank[:], lt[:, kc, :], wt[:, kc, :], kc == 0, kc == KC - 1) for kc in range(KC)]
                for ci, (k0, k1) in enumerate(kch):
                    K.op("pe", I_mm(mm[k0:k1]), reads=lres + [wrl[ci]], writes=[bres] if ci == 0 else [],
                         merge=[] if ci == 0 else [bres])
                epilogue(sc, i, fb, tt, bank, bres)

    class TileRing:
        def __init__(self, sc, n, shape, dt, nm):
            self.s = [sc.sbr(nm, shape, dt) for _ in range(n)]
            self.n = n
            self.nm = nm

        def __call__(self, i):
            t, r = self.s[i % self.n]
            return t, r, f"{self.nm}{i % self.n}"

    def ffn(xs, xs_res, xd, xd_res, gain, wg, wu, wdn, nm, hook=None):
        with K.scope() as sc:
            hT = sc.sb("hT", [128, 16, S], BF16)
            hT_res = [sc.res() for _ in range(16)]
            norm_T(xs, xs_res, gain, hT, hT_res, nm + "n")
            gT_res = [sc.res() for _ in range(16)]
            wd0 = (sc.sb("wd0", [128, 44, 512], BF16), [sc.res() for _ in range(4)])
            with K.scope() as sb:
                wr = WRing(K, sb, 8, nm)
                s32 = [sb.sbr("s32", [128, 512], F32) for _ in range(2)]
                grow = [sb.sbr("grow", [128, S], BF16) for _ in range(2)]
                groups = [[(wg, j * 128, 16, hT, lambda t4: hT_res[4 * t4:4 * t4 + 4]),
                           (wu, j * 128, 16, hT, lambda t4: hT_res[4 * t4:4 * t4 + 4])] for j in range(44)]
                cnt = [0]

                def epi(j, t4, ps):
                    (bg, rg), (bu, ru) = ps
                    b = cnt[0] % 2
                    cnt[0] += 1
                    st, sr = s32[b]
                    gt, gr = grow[j % 2]
                    if j == 34 and t4 == 0:
                        tm_loadw(wdn, 44, nm + "d", wd0[0], wd0[1], 0)
                    K.op("act", I_act(st[:], bg[:], AF.Silu), reads=[rg], writes=[sr])
                    K.op("dve", I_tt(gt[:, t4 * 512:(t4 + 1) * 512], st[:], bu[:], ALU.mult),
                         reads=[sr, ru], writes=[], merge=[gr])
                    if t4 == 3:
                        K.dma("sp", f"{nm}grow{j % 2}", gT[:, :, j, :].rearrange("t p c -> p t c"),
                              gt[:].rearrange("p (t c) -> p t c", c=128), reads=[gr], merge=gT_res)

                interleave(gemm_fm_gen(K, wr, groups, epi), hook(sb) if hook else None, 1)
            rings = {}

            def pre(sc2, i, fb, tt):
                if "g" not in rings:
                    rings["g"] = TileRing(sc2, 3, [128, 44, 128], BF16, nm + "gt")
                    rings["x"] = TileRing(sc2, 3, [128, 512], F32, nm + "xr")
                    rings["o"] = TileRing(sc2, 3, [128, 512], F32, nm + "xo")
                t, r, key = rings["g"](i)
                K.dma("sp", key, t[:], gT[tt], reads=[gT_res[tt]], writes=[r])
                t, r, key = rings["x"](i)
                K.dma("sp", key, t[:], xs[tt * 128:(tt + 1) * 128, fb * 512:(fb + 1) * 512],
                      reads=[xs_res[tt]] if xs_res else [], writes=[r])

            def lhs_get(i, fb, tt):
                t, r, _ = rings["g"](i)
                return t, [r]

            def epi2(sc2, i, fb, tt, bank, bres):
                xt, xr, _ = rings["x"](i)
                ot, orr, key = rings["o"](i)
                K.op("dve", I_stt(ot[:], bank[:], 0.5, xt[:], ALU.mult, ALU.add), reads=[bres, xr], writes=[orr])
                K.dma("sp", key, xd[tt * 128:(tt + 1) * 128, fb * 512:(fb + 1) * 512], ot[:],
                      reads=[orr], merge=[xd_res[tt]])

            gemm_tm(wdn, 44, lhs_get, pre, epi2, nm + "d", wd0=wd0)

    def fm_rownorm(src, src_r, gain, gain_r, mulrow, mul_r, dst, dst_r, sq, sq_r, rstd, rstd_r,
                   tmp, tmp_r, ones, ones_r, n):
        K.op("act", I_act(sq[:], src[:], AF.Square), reads=[src_r], writes=[sq_r])
        for t4 in range(4):
            sl = slice(t4 * 512, (t4 + 1) * 512)
            bank, bres = K.bank()
            K.op("pe", I_mm([(bank[:], ones[:], sq[:, sl], True, True)]), reads=[ones_r, sq_r], writes=[bres])
            K.op("act", I_act(rstd[:, sl], bank[:], AF.Ln, bias=float(n * EPS)), reads=[bres], writes=[],
                 merge=[rstd_r])
        K.op("act", I_act(rstd[:], rstd[:], AF.Exp, scale=-0.5), reads=[rstd_r], writes=[rstd_r])
        if mulrow is None:
            K.op("dve", I_stt(dst[:], src[:], gain, rstd[:], ALU.mult, ALU.mult),
                 reads=[src_r, gain_r, rstd_r], writes=[dst_r])
        else:
            K.op("dve", I_stt(tmp[:], src[:], gain, rstd[:], ALU.mult, ALU.mult),
                 reads=[src_r, gain_r, rstd_r], writes=[tmp_r])
            K.op("dve", I_tt(dst[:], tmp[:], mulrow[:], ALU.mult), reads=[tmp_r, mul_r], writes=[dst_r])

    def transpose_row(row, row_r, dst, dst_r, bankfn=None):
        for half in range(2):
            bank, bres = (bankfn or K.bank)()
            pb = bank[:].bitcast(BF16)
            K.op("pe", I_tr([(pb[:, j * 128:(j + 1) * 128],
                              row[:, (half * 8 + j) * 128:(half * 8 + j + 1) * 128]) for j in range(8)],
                            ident[:]), reads=[row_r, ident_r], writes=[bres])
            copy_on(K, "act" if half == 0 else "dve", dst[:, half * 8:(half + 1) * 8, :],
                    pb.rearrange("p (j c) -> p j c", j=8), [bres], [], merge=[dst_r])

    def colvec(sc, ap1n, n, nm, mul=None, reps=1):
        t, r = sc.sbr(nm, [n * reps, 1], F32)
        for i in range(reps):
            K.dma("sp", nm, t[i * n:(i + 1) * n, :], ap1n.rearrange("o n -> n o"), merge=[r])
        if mul is not None:
            K.op("dve", I_ts(t[:], t[:], float(mul), None, ALU.mult), reads=[r], writes=[r])
        return t, r

    def mixer(xs, xs_res, xd, xd_res):
        w_in = W["w_in"]
        mT_res = [top.res() for _ in range(16)]
        with K.scope() as sc:
            hT = sc.sb("h2T", [128, 16, S], BF16)
            hT_res = [sc.res() for _ in range(16)]
            norm_T(xs, xs_res, V["mix_norm"], hT, hT_res, "mn")
            hres4 = lambda t4: hT_res[4 * t4:4 * t4 + 4]
            ya_res = [sc.res() for _ in range(8)]
            yb_res = [sc.res() for _ in range(8)]
            ones, ones_r = sc.sbr("ones", [128, 128], BF16)
            K.op("dve", I_memset(ones[:], 1.0), writes=[ones_r])
            blk, blk_r = sc.sbr("blk", [128, 128], BF16)
            K.dma("sp", "c_blk", blk[:], c_blk64, writes=[blk_r])

            with K.scope() as sa:
                wr = WRing(K, sa, 6, "at")
                qg8, qg8_r = colvec(sa, V["q_norm"], 64, "qg8", 8.0, reps=2)
                kg8, kg8_r = colvec(sa, V["k_norm"], 64, "kg8", 8.0, reps=2)
                sg, sg_r = colvec(sa, V["diff_subln"], 128, "sg", math.sqrt(128.0) * (1.0 - LAMBDA_INIT))
                cball, cball_r = sa.sbr("cball", [128, 8], F32)
                K.dma("sp", "cball", cball[:], rel_bias[31].partition_broadcast(128), writes=[cball_r])
                lv = []
                for nm in ("lambda_q1", "lambda_k1", "lambda_q2", "lambda_k2"):
                    t, r = sa.sbr(nm, [128, 64], F32)
                    K.dma("sp", nm, t[:], V[nm][0].partition_broadcast(128), writes=[r])
                    lv.append((t, r))
                e12 = []
                for a, bb in ((0, 1), (2, 3)):
                    pr, pr_r = sa.sbr("lpr", [128, 64], F32)
                    sm, sm_r = sa.sbr("lsm", [128, 1], F32)
                    K.op("dve", I_tt(pr[:], lv[a][0][:], lv[bb][0][:], ALU.mult), reads=[lv[a][1], lv[bb][1]],
                         writes=[pr_r])
                    K.op("dve", (lambda pr=pr, sm=sm: lambda eng: eng.reduce_sum(out=sm[:], in_=pr[:], axis=AX.X))(),
                         reads=[pr_r], writes=[sm_r])
                    K.op("act", I_act(sm[:], sm[:], AF.Exp), reads=[sm_r], writes=[sm_r])
                    e12.append((sm, sm_r))
                neglam, neglam_r = sa.sbr("neglam", [128, 1], F32)
                K.op("dve", I_tt(neglam[:], e12[1][0][:], e12[0][0][:], ALU.subtract),
                     reads=[e12[0][1], e12[1][1]], writes=[neglam_r])
                K.op("dve", I_ts(neglam[:], neglam[:], -LAMBDA_INIT, None, ALU.add), reads=[neglam_r],
                     writes=[neglam_r])
                if SUB <= 1:
                    return
                q32, q32_r = sa.sbr("q32", [128, S], F32)
                rstd, rstd_r = sa.sbr("rstd", [128, S], F32)
                rstd2, rstd2_r = sa.sbr("rstd2", [128, S], F32)
                sq, sq_r = sa.sbr("sq", [128, S], BF16)
                yrow, yrow_r = sa.sbr("yrow", [128, S], BF16)
                vT, vT_r = sa.sbr("vT", [128, S], BF16)
                hb = [dict(qn=sa.sbr("qn", [128, S], BF16), kn=sa.sbr("kn", [128, S], BF16),
                           Vh=sa.sbr("Vh", [128, 16, 128], BF16)) for _ in range(2)]
                o32, o32_r = sa.sbr("o32", [128, S], F32)
                TTs = [sa.sbr("TT", [128, 1024], F32) for _ in range(2)]
                TT8s = [sa.sbr("TT8", [128, 1024], BF16) for _ in range(2)]
                zcol, zcol_r = sa.sbr("zcol", [128, 1], F32)
                K.op("dve", I_memset(zcol[:], 0.0), writes=[zcol_r])
                Ps = [sa.sbr("P", [128, 2, 512], BF16) for _ in range(3)]
                ep = [sa.sbr("ep", [128, 512], F32) for _ in range(4)]
                cn = {"s": 0, "p": 0, "sb": 0, "pb": 0}
                sq4 = [sa.res() for _ in range(4)]
                q324 = [sa.res() for _ in range(4)]
                rstd4 = [sa.res() for _ in range(4)]
                Laccs = [sa.sbr("Lacc", [128, 2, 512], F32) for _ in range(2)]
                LaccB, LaccB_r = sa.sbr("LaccB", [128, 2, 512], BF16)

                def proj_bank():
                    b = K.banks[6 + cn["pb"] % 2]
                    cn["pb"] += 1
                    return b

                def att_proj(h):
                    qn, qn_r = hb[h % 2]["qn"]
                    kn, kn_r = hb[h % 2]["kn"]
                    Vh, Vh_r = hb[h % 2]["Vh"]
                    TT, TT_r = TTs[h % 2]
                    K.dma("sp", f"TT{h % 2}", TT[:],
                          bass.AP(BV.tensor, h * 128 * BVL + 127, [[BVL - 1, 128], [1, 1024]]),
                          reads=[BV_res[h]], writes=[TT_r])
                    TT8, TT8_r = TT8s[h % 2]
                    K.op("dve", I_ts(TT8[:], TT[:], 8.0, None, ALU.mult), reads=[TT_r], writes=[TT8_r])
                    groups = [[(w_in, h * 128, 16, hT, hres4)], [(w_in, 1024 + h * 128, 16, hT, hres4)],
                              [(w_in, 2048 + h * 128, 16, hT, hres4)]]
                    pendB = []

                    def epi(g, t4, ps):
                        (bank, bres), = ps
                        sl = slice(t4 * 512, (t4 + 1) * 512)
                        if g == 2:
                            while pendB:
                                pendB.pop(0)()
                            copy_on(K, "act", vT[:, sl], bank[:], [bres], [], merge=[vT_r])
                            if t4 == 3:
                                transpose_row(vT, vT_r, Vh, Vh_r)
                            return
                        dst, dst_r, gn, gn_r = (qn, qn_r, qg8, qg8_r) if g == 0 else (kn, kn_r, kg8, kg8_r)
                        K.op("act", I_act(sq[:, sl], bank[:], AF.Square), reads=[bres], writes=[sq4[t4]])
                        K.op("dve", I_copy(q32[:, sl], bank[:]), reads=[bres], writes=[q324[t4]])

                        def stageB(sl=sl, dst=dst, dst_r=dst_r, gn=gn, gn_r=gn_r, t4=t4):
                            b2, b2r = K.bank()
                            K.op("pe", I_mm([(b2[:], blk[:], sq[:, sl], True, True)]), reads=[blk_r, sq4[t4]],
                                 writes=[b2r])
                            K.op("act", I_act(rstd[:, sl], b2[:], AF.Ln, bias=float(64 * EPS)), reads=[b2r],
                                 writes=[rstd4[t4]])
                            K.op("act", I_act(rstd[:, sl], rstd[:, sl], AF.Exp, scale=-0.5), reads=[rstd4[t4]],
                                 writes=[rstd4[t4]])
                            K.op("dve", I_stt(dst[:, sl], q32[:, sl], gn[:], rstd[:, sl], ALU.mult, ALU.mult),
                                 reads=[q324[t4], gn_r, rstd4[t4]], writes=[], merge=[dst_r])

                        while pendB:
                            pendB.pop(0)()
                        pendB.append(stageB)

                    yield from gemm_fm_gen(K, wr, groups, epi, prime=True)
                    while pendB:
                        pendB.pop(0)()
                    yield

                def bc2(ap, n):
                    return bass.AP(ap.tensor, ap.offset, [list(ap.ap[0]), [0, 2], list(ap.ap[1])])

                def att_loop(h):
                    qn, qn_r = hb[h % 2]["qn"]
                    kn, kn_r = hb[h % 2]["kn"]
                    Vh, Vh_r = hb[h % 2]["Vh"]
                    TT8, TT8_r = TT8s[h % 2]
                    acc = [K.banks[4], K.banks[6], K.banks[5], K.banks[7]]
                    for qt in range(4):
                        nkb = 4 * qt + 4
                        Lacc, Lacc_r = Laccs[qt % 2]
                        Sb = {}

                        def emit_S(kb, qt=qt):
                            c0 = max(0, 128 * (kb - 4 * qt))
                            p = cn["sb"] % 2
                            cn["sb"] += 1
                            near = (kb - 4 * qt) >= -1
                            o0 = 384 - 128 * (kb - 4 * qt)
                            for c in range(2):
                                bS, bSr = K.banks[2 * p + c]
                                mm = [(bS[:, c0:512], kn[c * 64:(c + 1) * 64, kb * 128:(kb + 1) * 128],
                                       qn[c * 64:(c + 1) * 64, qt * 512 + c0:(qt + 1) * 512], True, not near)]
                                if near:
                                    mm.append((bS[:, c0:512], ident[:], TT8[:, o0 + c0:o0 + 512], False, True))
                                K.op("pe", I_mm(mm), reads=[kn_r, qn_r] + ([ident_r, TT8_r] if near else []),
                                     writes=[bSr])
                            Sb[kb] = p

                        emit_S(0)
                        for kb in range(nkb):
                            if kb + 1 < nkb:
                                emit_S(kb + 1)
                            i = kb - 4 * qt
                            c0 = max(0, 128 * i)
                            p = Sb.pop(kb)
                            bres2 = [K.banks[2 * p][1], K.banks[2 * p + 1][1]]
                            S2 = K.ps[:, 2 * p:2 * p + 2, c0:512]
                            P, P_r = Ps[cn["p"] % 3]
                            cn["p"] += 1
                            if i <= -2:
                                K.op("act", I_act(P[:, :, c0:512], S2, AF.Exp, bias=cball[:, h:h + 1],
                                                  scale=0.125), reads=bres2 + [cball_r], writes=[P_r])
                            else:
                                K.op("act", I_act(P[:, :, c0:512], S2, AF.Exp, bias=zcol[:], scale=0.125),
                                     reads=bres2 + [zcol_r], writes=[P_r])
                            for c in range(2):
                                (bO, bOr) = acc[2 * c]
                                K.op("pe", I_mm([(bO[:, c0:512], Vh[:, kb, :], P[:, c, c0:512], kb == 0,
                                                  kb == nkb - 1)]),
                                     reads=[Vh_r, P_r], writes=[] if kb else [bOr], merge=[bOr] if kb else [])
                            if i <= -2:
                                for c in range(2):
                                    bL, bLr = acc[2 * c + 1]
                                    K.op("pe", I_mm([(bL[:], ones[:], P[:, c, :], kb == 0, False)]),
                                         reads=[ones_r, P_r], writes=[] if kb else [bLr], merge=[bLr] if kb else [])
                            elif i == (0 if qt == 0 else -1):
                                K.op("dve", I_copy(Lacc[:], P[:]), reads=[P_r], writes=[Lacc_r])
                            else:
                                K.op("dve", I_tt(Lacc[:, :, c0:512], Lacc[:, :, c0:512], P[:, :, c0:512], ALU.add),
                                     reads=[P_r, Lacc_r], writes=[Lacc_r])
                            yield
                        K.op("dve", I_copy(LaccB[:], Lacc[:]), reads=[Lacc_r], writes=[LaccB_r])
                        for c in range(2):
                            bL, bLr = acc[2 * c + 1]
                            K.op("pe", I_mm([(bL[:], ones[:], LaccB[:, c, :], qt == 0, True)]), reads=[ones_r, LaccB_r],
                                 writes=[bLr] if qt == 0 else [], merge=[] if qt == 0 else [bLr])
                        (bO1, rO1), (bL1, rL1), (bO2, rO2), (bL2, rL2) = acc
                        (r1, r1r), (r2, r2r), (o1, o1r), (t2, t2r) = ep
                        sl = slice(qt * 512, (qt + 1) * 512)
                        K.op("act", I_act(r1[:], bL1[:], AF.Ln), reads=[rL1], writes=[r1r])
                        K.op("act", I_act(r2[:], bL2[:], AF.Ln), reads=[rL2], writes=[r2r])
                        K.op("act", I_act(r1[:], r1[:], AF.Exp, scale=-1.0), reads=[r1r], writes=[r1r])
                        K.op("act", I_act(r2[:], r2[:], AF.Exp, scale=-1.0), reads=[r2r], writes=[r2r])
                        K.op("dve", I_tt(o1[:], bO1[:], r1[:], ALU.mult), reads=[rO1, r1r], writes=[o1r])
                        K.op("dve", I_stt(t2[:], bO2[:], neglam[:], r2[:], ALU.mult, ALU.mult),
                             reads=[rO2, neglam_r, r2r], writes=[t2r])
                        K.op("dve", I_tt(o32[:, sl], o1[:], t2[:], ALU.add), reads=[o1r, t2r], writes=[],
                             merge=[o32_r])
                        yield
                    if SUBA <= 3:
                        return
                    fm_rownorm(o32, o32_r, sg[:], sg_r, None, None, yrow, yrow_r, yrow, yrow_r, rstd2, rstd2_r,
                               None, None, ones, ones_r, 128)
                    K.dma("sp", "yast", yaT[h], yrow[:], reads=[yrow_r], writes=[ya_res[h]])
                    yield

                gp = att_proj(0)
                next(gp)
                for h in range(NH):
                    for _ in gp:
                        pass
                    if h + 1 < NH:
                        gp = att_proj(h + 1)
                        next(gp)
                    for _ in att_loop(h):
                        pass

            if SUB <= 2:
                return
            with K.scope() as sh:
                wr = WRing(K, sh, 6, "hg")
                hgn, hgn_r = colvec(sh, V["hgrn_norm"], 128, "hgn", math.sqrt(128.0))
                lg, lg_r = sh.sbr("lg", [128, 2, 8], F32)
                for r_ in range(2):
                    for h_ in range(8):
                        K.dma("sp", "lg", lg[:, r_, h_:h_ + 1],
                              lb_logits[r_:r_ + 1, h_ * 128:(h_ + 1) * 128].rearrange("o n -> n o"), merge=[lg_r])
                lb, lb_r = sh.sbr("lb", [128, 8], F32)
                oml, oml_r = sh.sbr("oml", [128, 8], F32)
                K.op("dve", I_tt(lb[:], lg[:, 0, :], lg[:, 1, :], ALU.subtract), reads=[lg_r], writes=[lb_r])
                K.op("act", I_act(lb[:], lb[:], AF.Sigmoid), reads=[lb_r], writes=[lb_r])
                K.op("dve", I_ts(oml[:], lb[:], -1.0, 1.0, ALU.mult, ALU.add), reads=[lb_r], writes=[oml_r])
                scm, scm_r = sh.sbr("scm", [128, S], F32)
                K.dma("sp", "scm", scm[:], c_scan[0].partition_broadcast(128), writes=[scm_r])
                mk2, mk2_r = sh.sbr("mk2", [128, 128], F32)
                K.dma("sp", "mk2", mk2[:], c_mask2, writes=[mk2_r])
                hs = [dict(R1=sh.sbr("R1", [128, S], F32),
                           R2=sh.sbr("R2", [128, S], F32),
                           viT=sh.sbr("viT", [128, S], BF16),
                           og=sh.sbr("og", [128, S], BF16),
                           Vh=sh.sbr("Vhh", [128, 16, 128], BF16)) for _ in range(2)]
                R3, R3r = sh.sbr("R3", [128, S], F32)
                R4, R4r = sh.sbr("R4", [128, S], F32)
                R5, R5r = sh.sbr("R5", [128, S], F32)
                Qt, Qt_r = sh.sbr("Qt", [128, S], BF16)
                Kt, Kt_r = sh.sbr("Kt", [128, S], BF16)
                Kh, Kh_r = sh.sbr("Kh", [128, S], BF16)
                KhT, KhT_r = sh.sbr("KhT", [128, 16, 128], BF16)
                st32 = [sh.sbr("st32", [128, 128], F32) for _ in range(2)]
                st16 = [sh.sbr("st16", [128, 128], BF16) for _ in range(6)]
                STm = [sh.sbr("STm", [128, 128], BF16) for _ in range(2)]

                def hg_proj(h):
                    st = hs[h % 2]
                    (R1, R1r), (R2, R2r), (viT, viT_r), (og, og_r), (Vh, Vh_r) = (st["R1"], st["R2"], st["viT"],
                                                                                   st["og"], st["Vh"])
                    order = [2, 0, 1, 3]
                    groups = [[(w_in, (3 + g) * 1024 + h * 128, 16, hT, hres4)] for g in order]

                    def epi(gi, t4, ps):
                        g = order[gi]
                        (bank, bres), = ps
                        sl = slice(t4 * 512, (t4 + 1) * 512)
                        if g == 0:
                            K.op("act", I_act(R1[:, sl], bank[:], AF.Silu), reads=[bres], writes=[], merge=[R1r])
                        elif g == 1:
                            K.op("act", I_act(R2[:, sl], bank[:], AF.Sigmoid), reads=[bres], writes=[], merge=[R2r])
                        elif g == 2:
                            copy_on(K, "dve", viT[:, sl], bank[:], [bres], [], merge=[viT_r])
                            if t4 == 3:
                                transpose_row(viT, viT_r, Vh, Vh_r)
                        else:
                            K.op("act", I_act(og[:, sl], bank[:], AF.Silu), reads=[bres], writes=[], merge=[og_r])

                    yield from gemm_fm_gen(K, wr, groups, epi, split=2)

                def hg_main(h):
                    st = hs[h % 2]
                    (R1, R1r), (R2, R2r), (viT, viT_r), (og, og_r), (Vh, Vh_r) = (st["R1"], st["R2"], st["viT"],
                                                                                   st["og"], st["Vh"])
                    K.op("dve", I_ts(R2[:], R2[:], oml[:, h:h + 1], lb[:, h:h + 1], ALU.mult, ALU.add),
                         reads=[R2r, oml_r, lb_r], writes=[R2r])
                    yield
                    K.op("act", I_act(R3[:], R2[:], AF.Ln), reads=[R2r], writes=[R3r])
                    K.op("dve", I_ts(R2[:], R2[:], -1.0, 1.0, ALU.mult, ALU.add), reads=[R2r], writes=[R2r])
                    yield
                    K.op("dve", lambda eng: eng.tensor_tensor_scan(out=R4[:], data0=scm[:], data1=R3[:], initial=0.0,
                                                                   op0=ALU.mult, op1=ALU.add),
                         reads=[scm_r, R3r], writes=[R4r])
                    K.op("act", I_act(R5[:], R4[:], AF.Exp), reads=[R4r], writes=[R5r])
                    yield
                    K.op("act", I_act(R3[:], R4[:], AF.Exp, scale=-1.0), reads=[R4r], writes=[R3r])
                    K.op("dve", I_tt(Qt[:], R1[:], R5[:], ALU.mult), reads=[R1r, R5r], writes=[Qt_r])
                    yield
                    K.op("dve", I_tt(Kt[:], R2[:], R3[:], ALU.mult), reads=[R2r, R3r], writes=[Kt_r])
                    b3 = R4[:].rearrange("p (c t) -> p c t", t=64)
                    K.op("dve", I_tt(R3[:].rearrange("p (c t) -> p c t", t=64),
                                     b3[:, :, 63:64].to_broadcast([128, 32, 64]), b3, ALU.subtract),
                         reads=[R4r], writes=[R3r])
                    yield
                    K.op("act", I_act(R3[:], R3[:], AF.Exp), reads=[R3r], writes=[R3r])
                    K.op("dve", I_tt(Kh[:], R2[:], R3[:], ALU.mult), reads=[R2r, R3r], writes=[Kh_r])
                    transpose_row(Kh, Kh_r, KhT, KhT_r)
                    yield
                    e3 = R5[:].rearrange("p (c t) -> p c t", t=64)
                    K.op("dve", I_memset(st32[0][0][:], 0.0), writes=[st32[0][1]])
                    K.op("dve", I_memset(st16[0][0][:], 0.0), writes=[st16[0][1]])
                    pre_ = {}

                    def emit_pre(j):
                        tk = slice(j * 128, (j + 1) * 128)
                        bST, bSTr = K.bank()
                        K.op("pe", I_mm([(bST[:, 0:128], Kt[:, tk], Qt[:, tk], True, True)]), reads=[Kt_r, Qt_r],
                             writes=[bSTr])
                        bU = [K.bank(), K.bank()]
                        for u in range(2):
                            K.op("pe", I_mm([(bU[u][0][:, 0:128], KhT[u * 64:(u + 1) * 64, j, :],
                                              Vh[u * 64:(u + 1) * 64, j, :], True, True)]),
                                 reads=[KhT_r, Vh_r], writes=[bU[u][1]])
                        pre_[j] = (bST, bSTr, bU)

                    emit_pre(0)
                    for j in range(16):
                        tk = slice(j * 128, (j + 1) * 128)
                        bST, bSTr, bU = pre_.pop(j)
                        sm, sm_r = STm[j % 2]
                        K.op("dve", I_tt(sm[:], bST[:, 0:128], mk2[:], ALU.mult), reads=[bSTr, mk2_r], writes=[sm_r])
                        s16 = [st16[(2 * j) % 6], st16[(2 * j + 1) % 6], st16[(2 * j + 2) % 6]]
                        s32 = [st32[0], st32[1], st32[0]]
                        for u in range(2):
                            K.op("dve", I_stt(s32[u + 1][0][:], s32[u][0][:], e3[:, 2 * j + u, 63:64], bU[u][0][:, 0:128],
                                              ALU.mult, ALU.add), reads=[s32[u][1], R5r, bU[u][1]], writes=[s32[u + 1][1]])
                            copy_on(K, "act", s16[u + 1][0][:], s32[u + 1][0][:], [s32[u + 1][1]], [s16[u + 1][1]])
                        if j + 1 < 16:
                            emit_pre(j + 1)
                        bO, bOr = K.bank()
                        K.op("pe", I_mm([(bO[:, 0:128], Vh[:, j, :], sm[:], True, False),
                                         (bO[:, 0:64], s16[0][0][:], Qt[:, j * 128:j * 128 + 64], False, False),
                                         (bO[:, 64:128], s16[1][0][:], Qt[:, j * 128 + 64:(j + 1) * 128], False, True)]),
                             reads=[Vh_r, sm_r, s16[0][1], s16[1][1], Qt_r], writes=[bOr])
                        copy_on(K, "act", R1[:, tk], bO[:, 0:128], [bOr], [], merge=[R1r])
                        yield
                    fm_rownorm(R1, R1r, hgn[:], hgn_r, og, og_r, Kh, Kh_r, viT, viT_r, R4, R4r, R3, R3r,
                               ones, ones_r, 128)
                    K.dma("sp", "ybst", ybT[h], Kh[:], reads=[Kh_r], writes=[yb_res[h]])
                    yield

                for _ in hg_proj(0):
                    pass
                for h in range(8):
                    interleave(hg_main(h), hg_proj(h + 1) if h + 1 < 8 else None, 2)

            if SUB <= 3:
                return
            with K.scope() as sm_:
                wr = WRing(K, sm_, 8, "mg")
                yas = sm_.sb("yas", [128, 8, S], BF16)
                ybs = sm_.sb("ybs", [128, 8, S], BF16)
                yas_r = [sm_.res() for _ in range(8)]
                ybs_r = [sm_.res() for _ in range(8)]
                for h in range(8):
                    K.dma("sp", "yald", yas[:, h, :], yaT[h], reads=[ya_res[h]], writes=[yas_r[h]])
                    K.dma("sp", "ybld", ybs[:, h, :], ybT[h], reads=[yb_res[h]], writes=[ybs_r[h]])
                sg_ = [sm_.sbr("sga", [128, 512], F32) for _ in range(2)]
                sb_ = [sm_.sbr("sgb", [128, 512], F32) for _ in range(2)]
                mrow = [sm_.sbr("mrow", [128, S], BF16) for _ in range(2)]
                groups = [[(w_in, 7168 + m * 128, 16, hT, hres4), (w_in, 9216 + m * 128, 16, hT, hres4),
                           (W["w_branch_a"], m * 128, 8, yas, lambda t4: yas_r),
                           (W["w_branch_b"], m * 128, 8, ybs, lambda t4: ybs_r)] for m in range(16)]
                cnt = [0]

                def epi(m, t4, ps):
                    (bga, rga), (bgb, rgb), (bba, rba), (bbb, rbb) = ps
                    b = cnt[0] % 2
                    cnt[0] += 1
                    (ta, tar), (tb, tbr) = sg_[b], sb_[b]
                    mt, mr = mrow[m % 2]
                    sl = slice(t4 * 512, (t4 + 1) * 512)
                    K.op("act", I_act(ta[:], bga[:], AF.Sigmoid), reads=[rga], writes=[tar])
                    K.op("act", I_act(tb[:], bgb[:], AF.Sigmoid), reads=[rgb], writes=[tbr])
                    K.op("dve", I_tt(ta[:], ta[:], bba[:], ALU.mult), reads=[tar, rba], writes=[tar])
                    K.op("dve", I_tt(tb[:], tb[:], bbb[:], ALU.mult), reads=[tbr, rbb], writes=[tbr])
                    K.op("dve", I_tt(mt[:, sl], ta[:], tb[:], ALU.add), reads=[tar, tbr], writes=[], merge=[mr])
                    if t4 == 3:
                        K.dma("sp", f"mrow{m % 2}", mT[:, :, m, :].rearrange("t p c -> p t c"),
                              mt[:].rearrange("p (t c) -> p t c", c=128), reads=[mr], merge=mT_res)

                gemm_fm(K, wr, groups, epi)

        if SUB <= 4:
            return
        rings = {}

        def pre(sc2, i, fb, tt):
            if "g" not in rings:
                rings["g"] = [sc2.sbr("wo_m", [128, 16, 128], BF16) for _ in range(16)]
                for t_ in range(16):
                    K.dma("sp", f"wo_m{t_ % 4}", rings["g"][t_][0][:], mT[t_], reads=[mT_res[t_]],
                          writes=[rings["g"][t_][1]])
                rings["x"] = TileRing(sc2, 3, [128, 512], F32, "wo_xr")
                rings["o"] = TileRing(sc2, 3, [128, 512], F32, "wo_xo")
            t, r, key = rings["x"](i)
            K.dma("sp", key, t[:], xs[tt * 128:(tt + 1) * 128, fb * 512:(fb + 1) * 512],
                  reads=[xs_res[tt]], writes=[r])

        def lhs_get(i, fb, tt):
            t, r = rings["g"][tt]
            return t, [r]

        def epi2(sc2, i, fb, tt, bank, bres):
            xt, xr, _ = rings["x"](i)
            ot, orr, key = rings["o"](i)
            K.op("dve", I_tt(ot[:], bank[:], xt[:], ALU.add), reads=[bres, xr], writes=[orr])
            K.dma("sp", key, xd[tt * 128:(tt + 1) * 128, fb * 512:(fb + 1) * 512], ot[:],
                  reads=[orr], merge=[xd_res[tt]])

        gemm_tm(W["w_out"], 16, lhs_get, pre, epi2, "wo")

    def ple_proj_gen(sc, ple_res):
        if True:
            wp, wp_r = sc.sbr("wp", [128, 2, D], BF16)
            K.dma("pool", "wp", wp[:], wview(W["w_ple_proj"]), writes=[wp_r])
            gb, gb_r = sc.sbr("pgb", [128, D], F32)
            K.dma("sp", "pgb", gb[:], V["ple_post_norm"][0].partition_broadcast(128), writes=[gb_r])
            K.op("dve", I_ts(gb[:], gb[:], math.sqrt(D), None, ALU.mult), reads=[gb_r], writes=[gb_r])
            pin = [sc.sbr("pin", [128, 256], F32) for _ in range(2)]
            pbf = [sc.sbr("pbf", [128, 256], BF16) for _ in range(2)]
            pT = [sc.sbr("pT", [128, 2, 128], BF16) for _ in range(2)]
            pl32 = [sc.sbr("pl32", [128, D], F32) for _ in range(2)]
            junk, junk_r = sc.sbr("pjunk", [128, D], BF16)
            ss = [sc.sbr("pss", [128, 1], F32) for _ in range(2)]
            for tt in range(16):
                b = tt % 2
                K.dma("sp", f"pin{b}", pin[b][0][:], p_in[tt * 128:(tt + 1) * 128, :], writes=[pin[b][1]])
                K.op("dve", I_copy(pbf[b][0][:], pin[b][0][:]), reads=[pin[b][1]], writes=[pbf[b][1]])
                bank, bres = K.bank()
                pb = bank[:].bitcast(BF16)
                K.op("pe", I_tr([(pb[:, j * 128:(j + 1) * 128], pbf[b][0][:, j * 128:(j + 1) * 128])
                                 for j in range(2)], ident[:]), reads=[pbf[b][1], ident_r], writes=[bres])
                copy_on(K, "dve", pT[b][0][:], pb[:, 0:256].rearrange("p (j c) -> p j c", j=2), [bres], [pT[b][1]])
                for fb in range(4):
                    bank, bres = K.bank()
                    K.op("pe", I_mm([(bank[:], pT[b][0][:, kc, :], wp[:, kc, fb * 512:(fb + 1) * 512], kc == 0, kc == 1)
                                     for kc in range(2)]), reads=[pT[b][1], wp_r], writes=[bres])
                    copy_on(K, "act", pl32[b][0][:, fb * 512:(fb + 1) * 512], bank[:], [bres], [], merge=[pl32[b][1]])
                K.op("act", I_act(junk[:], pl32[b][0][:], AF.Square, accum=ss[b][0][:]), reads=[pl32[b][1]],
                     writes=[junk_r, ss[b][1]])
                rsqrt_act(K, ss[b][0][:], ss[b][0][:], D * EPS, [ss[b][1]], ss[b][1])
                K.op("dve", I_stt(pl32[b][0][:], pl32[b][0][:], ss[b][0][:], gb[:], ALU.mult, ALU.mult),
                     reads=[pl32[b][1], ss[b][1], gb_r], writes=[pl32[b][1]])
                K.dma("sp", f"plst{b}", ple_d[tt * 128:(tt + 1) * 128, :], pl32[b][0][:], reads=[pl32[b][1]],
                      writes=[ple_res[tt]])
                yield

    def ple_phase(xs, xs_res, xd, xd_res, ple_res):
        with K.scope() as sc:
            hT = sc.sb("h4T", [128, 16, S], BF16)
            hT_res = [sc.res() for _ in range(16)]
            norm_T(xs, xs_res, V["ple_gate_norm"], hT, hT_res, "pn")
            rings = {}

            def pre(sc2, i, fb, tt):
                if "x" not in rings:
                    rings["x"] = TileRing(sc2, 3, [128, 512], F32, "pg_xr")
                    rings["p"] = TileRing(sc2, 3, [128, 512], F32, "pg_pl")
                    rings["s"] = TileRing(sc2, 3, [128, 512], F32, "pg_sg")
                t, r, key = rings["x"](i)
                K.dma("sp", key, t[:], xs[tt * 128:(tt + 1) * 128, fb * 512:(fb + 1) * 512],
                      reads=[xs_res[tt]], writes=[r])
                t, r, key = rings["p"](i)
                K.dma("sp", key, t[:], ple_d[tt * 128:(tt + 1) * 128, fb * 512:(fb + 1) * 512],
                      reads=[ple_res[tt]], writes=[r])

            def lhs_get(i, fb, tt):
                return hT[:, :, tt * 128:(tt + 1) * 128], [hT_res[tt]]

            def epi2(sc2, i, fb, tt, bank, bres):
                xt, xr, _ = rings["x"](i)
                pt, pr, _ = rings["p"](i)
                st, sr, key = rings["s"](i)
                K.op("act", I_act(st[:], bank[:], AF.Sigmoid), reads=[bres], writes=[sr])
                K.op("dve", I_tt(st[:], st[:], pt[:], ALU.mult), reads=[sr, pr], writes=[sr])
                K.op("dve", I_tt(st[:], st[:], xt[:], ALU.add), reads=[sr, xr], writes=[sr])
                K.dma("sp", key, xd[tt * 128:(tt + 1) * 128, fb * 512:(fb + 1) * 512], st[:],
                      reads=[sr], merge=[xd_res[tt]])

            gemm_tm(W["w_ple_gate"], 16, lhs_get, pre, epi2, "pg")

    x1_res = [top.res() for _ in range(16)]
    x2_res = [top.res() for _ in range(16)]
    x3_res = [top.res() for _ in range(16)]
    out_res = [top.res() for _ in range(16)]

    ffn(x_in, None, x1, x1_res if stage > 1 else out_res, V["ffn1_norm"],
        W["ffn1_w_gate"], W["ffn1_w_up"], W["ffn1_w_down"], "f1")
    if stage > 1:
        mixer(x1, x1_res, x2, x2_res if stage > 2 else out_res)
    ple_res = [top.res() for _ in range(16)]

    def spread(gen, k):
        for _ in gen:
            for _k in range(k):
                yield

    if stage > 2:
        ffn(x2, x2_res, x3, x3_res if stage > 3 else out_res, V["ffn2_norm"],
            W["ffn2_w_gate"], W["ffn2_w_up"], W["ffn2_w_down"], "f2",
            hook=(lambda sc: spread(ple_proj_gen(sc, ple_res), 8)) if stage > 3 else None)
    if stage > 3:
        ple_phase(x3, x3_res, out, out_res, ple_res)

    K.wait_all("sp", out_res)
    K.emit()
    es.close()
    return nc


def _consts():
    bf = ml_dtypes.bfloat16
    ident = np.eye(128, dtype=np.float32).astype(bf)
    blk = np.zeros((128, 128), np.float32)
    blk[:64, :64] = 1
    blk[64:, 64:] = 1
    s = np.arange(128)[:, None]
    t = np.arange(128)[None, :]
    mask2 = ((s // 64 == t // 64) & (t >= s)).astype(np.float32)
    scan = np.ones((1, S), np.float32)
    scan[0, ::64] = 0
    oh = np.zeros((33, BVL), np.float32)
    for idx in range(BVL - 1):
        dist = idx - 511
        if dist < 0:
            oh[32, idx] = 1
        else:
            n = dist
            if n < 16:
                b = n
            else:
                nf = np.float32(max(n, 1))
                b = 16 + int(np.float32(np.log(nf / np.float32(16)) / np.float32(math.log(8.0))
                                        * np.float32(16)))
                b = min(b, 31)
            oh[b, idx] = 1
    return dict(c_ident=ident, c_blk64=blk.astype(bf), c_mask2=mask2, c_scan=scan, c_oh=oh)


_W2 = ("ffn1_w_gate", "ffn1_w_up", "ffn1_w_down", "w_in", "w_branch_a", "w_branch_b", "w_out",
       "ffn2_w_gate", "ffn2_w_up", "ffn2_w_down", "w_ple_gate", "w_ple_proj")
_V1 = ("ffn1_norm", "mix_norm", "ffn2_norm", "ple_gate_norm", "ple_post_norm", "q_norm", "k_norm",
       "lambda_q1", "lambda_k1", "lambda_q2", "lambda_k2", "diff_subln", "hgrn_norm")


def kernel(**inputs):
    stage = int(os.environ.get("MK_STAGE", "99"))
    debug = bool(int(os.environ.get("MK_DEBUG", "0")))
    nc = build(stage, debug)
    shared = {}
    for k in _W2:
        shared[k] = np.ascontiguousarray(np.asarray(inputs[k], np.float32)[0])
    for k in _V1:
        shared[k] = np.ascontiguousarray(np.asarray(inputs[k], np.float32))
    shared["rel_bias"] = np.ascontiguousarray(np.asarray(inputs["rel_bias"], np.float32))
    shared["hgrn_lb_logits"] = np.ascontiguousarray(np.asarray(inputs["hgrn_lb_logits"], np.float32))
    shared.update(_consts())
    x = np.asarray(inputs["x"], np.float32)
    p = np.asarray(inputs["p"], np.float32)
    in_maps = []
    ncores = int(os.environ.get("MK_CORES", NCORES))
    for c in range(ncores):
        m = dict(shared)
        m["x"] = np.ascontiguousarray(x[c])
        m["p"] = np.ascontiguousarray(p[0, c])
        in_maps.append(m)
    res = run_bass_kernel_spmd(nc, in_maps, core_ids=list(range(ncores)))
    if debug:
        kernel.last = res.results
    return np.stack([np.asarray(r["out"], np.float32) for r in res.results], axis=0)
```

```python
import os
import math
from contextlib import ExitStack, contextmanager
import numpy as np
import ml_dtypes
import concourse.bass as bass
import concourse.mybir as mybir
from concourse.bass_utils import run_bass_kernel_spmd

F32 = mybir.dt.float32
BF16 = mybir.dt.bfloat16
AF = mybir.ActivationFunctionType
ALU = mybir.AluOpType
AX = mybir.AxisListType

S = 2048
D = 2048
DFF = 5632
NIN = 11264
EPS = 1e-6
NCORES = 8
LAMBDA_INIT = 0.8 - 0.6 * math.exp(-0.3 * 0)
BVL = 1152
SUB = int(os.environ.get("MK_SUB", "9"))
SUBA = int(os.environ.get("MK_SUBA", "9"))
NH = int(os.environ.get("MK_NH", "8"))


class Res:
    __slots__ = ("w", "r", "isbank")

    def __init__(self, K, isbank=False):
        self.w = dict(K.grave)
        self.r = {}
        self.isbank = isbank


class Scope:
    def __init__(self, K):
        self.K = K
        self.es = ExitStack()
        self.rs = []

    def sb(self, name, shape, dt):
        self.K.uid += 1
        return self.es.enter_context(self.K.nc.sbuf_tensor(f"{name}_{self.K.uid}", shape, dt))

    def res(self):
        r = Res(self.K)
        self.rs.append(r)
        return r

    def sbr(self, name, shape, dt):
        return self.sb(name, shape, dt), self.res()


class Sched:
    def __init__(self, nc, es):
        self.nc = nc
        self.es = es
        self.engs = dict(pe=nc.tensor, act=nc.scalar, dve=nc.vector, pool=nc.gpsimd, sp=nc.sync)
        self.sem = {}
        self.cnt = {}
        self.prog = {}
        self.waited = {}
        self.grave = {}
        self.uid = 0
        for e in self.engs:
            self.sem["s_" + e] = es.enter_context(nc.semaphore("s_" + e))
            self.cnt["s_" + e] = 0
            self.prog[e] = []
            self.waited[e] = {}
        self.nphys = {"pool": 26, "sp": 64}
        self.k2p = {"pool": {}, "sp": {}}
        for q, n in self.nphys.items():
            for i in range(n):
                self.sem[f"{q}{i}"] = es.enter_context(nc.semaphore(f"{q}{i}"))
                self.cnt[f"{q}{i}"] = 0
        self.banks = []
        self.ps = es.enter_context(nc.psum_tensor("psall", [128, 8, 512], F32))
        for i in range(8):
            self.banks.append((self.ps[:, i, :], Res(self, True)))
        self.bi = 0

    def bank(self):
        b = self.banks[self.bi]
        self.bi = (self.bi + 1) % 8
        return b

    @contextmanager
    def scope(self):
        sc = Scope(self)
        try:
            yield sc
        finally:
            for r in sc.rs:
                for d in (r.w, r.r):
                    for k, v in d.items():
                        if self.grave.get(k, 0) < v:
                            self.grave[k] = v
            sc.es.close()

    def _need(self, e, reads, writes, merge):
        need = {}
        for r in reads:
            for k, v in r.w.items():
                if need.get(k, 0) < v:
                    need[k] = v
        for w in writes:
            for d in (w.w, w.r):
                for k, v in d.items():
                    if need.get(k, 0) < v:
                        need[k] = v
        for w in merge:
            for k, v in w.r.items():
                if need.get(k, 0) < v:
                    need[k] = v
        if e == "pe":
            need.pop("s_pe", None)
        wd = self.waited[e]
        waits = []
        for k, v in need.items():
            if wd.get(k, 0) < v:
                wd[k] = v
                waits.append((k, v))
        return waits

    def _mark(self, tok, reads, writes, merge):
        k, v = tok
        for r in reads:
            r.r[k] = v
        for w in writes:
            w.w = {k: v}
            w.r = {}
        for w in merge:
            w.w[k] = v

    def op(self, e, fn, reads=(), writes=(), merge=()):
        if e != "pe":
            br = [r for r in reads if r.isbank]
            if br:
                reads = [r for r in reads if not r.isbank]
                writes = list(writes) + br
        waits = self._need(e, reads, writes, merge)
        key = "s_" + e
        self.cnt[key] += 1
        tok = (key, self.cnt[key])
        self.prog[e].append((waits, fn, tok, 1))
        self._mark(tok, reads, writes, merge)

    def dma(self, q, key, out, in_, reads=(), writes=(), merge=(), slow=False):
        k2p = self.k2p[q]
        if key not in k2p:
            k2p[key] = f"{q}{len(k2p) % self.nphys[q]}"
        key = k2p[key]
        waits = self._need(q, reads, writes, merge)
        c = self.cnt[key]
        if c and self.waited[q].get(key, 0) < c:
            self.waited[q][key] = c
            waits.append((key, c))
        self.cnt[key] += 16
        tok = (key, self.cnt[key])
        if slow:
            fn = lambda eng: eng.dma_start(out=out, in_=in_, allow_slow_non_contiguous=True)
        else:
            fn = lambda eng: eng.dma_start(out=out, in_=in_)
        self.prog[q].append((waits, fn, tok, 16))
        self._mark(tok, reads, writes, merge)

    def wait_all(self, e, rs):
        waits = self._need(e, rs, (), ())
        self.prog[e].append((waits, None, None, 0))

    def emit(self):
        nc = self.nc
        with nc.Block() as block:
            for e, deco in (("sp", block.sync), ("act", block.scalar), ("dve", block.vector),
                            ("pool", block.gpsimd), ("pe", block.tensor)):
                prog = self.prog[e]

                def body(eng, prog=prog):
                    for waits, fn, tok, inc in prog:
                        for k, v in waits:
                            eng.wait_ge(self.sem[k], v)
                        if fn is not None:
                            fn(eng).then_inc(self.sem[tok[0]], inc)

                deco(body)


def I_act(out, in_, func, bias=None, scale=None, accum=None):
    def f(eng):
        kw = {}
        if bias is not None:
            kw["bias"] = bias
        if scale is not None:
            kw["scale"] = scale
        if accum is not None:
            kw["accum_out"] = accum
        return eng.activation(out=out, in_=in_, func=func, **kw)
    return f


def I_stt(out, in0, scalar, in1, op0, op1):
    return lambda eng: eng.scalar_tensor_tensor(out=out, in0=in0, scalar=scalar, in1=in1, op0=op0, op1=op1)


def I_ts(out, in0, s1, s2, op0, op1=None):
    if op1 is None:
        return lambda eng: eng.tensor_scalar(out=out, in0=in0, scalar1=s1, scalar2=None, op0=op0)
    return lambda eng: eng.tensor_scalar(out=out, in0=in0, scalar1=s1, scalar2=s2, op0=op0, op1=op1)


def I_tt(out, in0, in1, op):
    return lambda eng: eng.tensor_tensor(out=out, in0=in0, in1=in1, op=op)


def I_copy(out, in_):
    return lambda eng: eng.tensor_copy(out=out, in_=in_)


def I_acopy(out, in_):
    return lambda eng: eng.activation(out=out, in_=in_, func=AF.Copy)


def I_mm(lst):
    def f(eng):
        ins = None
        for (o, l, r, st, sp) in lst:
            ins = eng.matmul(o, lhsT=l, rhs=r, start=st, stop=sp)
        return ins
    return f


def I_tr(lst, ident):
    def f(eng):
        ins = None
        for (o, i) in lst:
            ins = eng.transpose(out=o, in_=i, identity=ident)
        return ins
    return f


def I_memset(ap, v):
    return lambda eng: eng.memset(ap, v)


def rsqrt_act(K, out, in_, add, reads, wres):
    K.op("act", I_act(out, in_, AF.Ln, bias=add), reads=reads, writes=[wres])
    K.op("act", I_act(out, out, AF.Exp, scale=-0.5), reads=[wres], writes=[wres])


def copy_on(K, e, out, in_, reads, writes, merge=()):
    K.op(e, I_acopy(out, in_) if e == "act" else I_copy(out, in_), reads=reads, writes=writes, merge=merge)


class WRing:
    def __init__(self, K, sc, n, nm):
        self.slots = []
        for i in range(n):
            t, r = sc.sbr(f"{nm}w{i}", [128, 16, 128], BF16)
            self.slots.append((t, r, f"{nm}w{i}"))
        self.i = 0
        self.n = n

    def next(self):
        s = self.slots[self.i]
        self.i = (self.i + 1) % self.n
        return s


def wview(W):
    return W.rearrange("(kc p) n -> p kc n", p=128)


def gemm_fm_gen(K, wr, groups, epilogue, bankfn=None, split=1, prime=False):
    bankfn = bankfn or K.bank
    ppg = max(len(g) for g in groups)
    ahead = max(1, wr.n // ppg - 1)
    pending = {}

    def issue(g):
        sl = []
        for (W, c0, KC, src, sres) in groups[g]:
            t, r, key = wr.next()
            K.dma("pool", key, t[:, 0:KC, :], wview(W)[:, :, c0:c0 + 128], writes=[r])
            sl.append((t, r))
        pending[g] = sl

    for g in range(min(ahead, len(groups))):
        issue(g)
    if prime:
        yield
    for g in range(len(groups)):
        if g + ahead < len(groups):
            issue(g + ahead)
        sl = pending.pop(g)
        for t4 in range(4):
            ps = []
            for (W, c0, KC, src, sres), (wt, wres) in zip(groups[g], sl):
                bank, bres = bankfn()
                mm = [(bank[:], wt[:, kc, :], src[:, kc, t4 * 512:(t4 + 1) * 512], kc == 0, kc == KC - 1)
                      for kc in range(KC)]
                step = (KC + split - 1) // split
                for si, k0 in enumerate(range(0, KC, step)):
                    K.op("pe", I_mm(mm[k0:k0 + step]), reads=[wres] + sres(t4),
                         writes=[bres] if si == 0 else [], merge=[] if si == 0 else [bres])
                    if split > 1:
                        yield
                ps.append((bank, bres))
            epilogue(g, t4, ps)
            yield


def gemm_fm(K, wr, groups, epilogue):
    for _ in gemm_fm_gen(K, wr, groups, epilogue):
        pass


def interleave(main, side, per):
    for _ in main:
        for _k in range(per):
            if side is not None:
                try:
                    next(side)
                except StopIteration:
                    side = None
    if side is not None:
        for _ in side:
            pass


def build(stage=99, debug=False):
    nc = bass.Bass("TRN2", target_bir_lowering=False)
    es = ExitStack()

    def din(name, shape, dt=F32):
        return nc.dram_tensor(name, shape, dt, kind="ExternalInput").ap()

    def dscr(name, shape, dt, dbg=False):
        if dbg and debug:
            return nc.dram_tensor(name, shape, dt, kind="ExternalOutput").ap()
        return nc.dram_tensor(name, shape, dt).ap()

    x_in = din("x", [S, D])
    p_in = din("p", [S, 256])
    out = nc.dram_tensor("out", [S, D], F32, kind="ExternalOutput").ap()
    W = {}
    for nm, shp in (("ffn1_w_gate", [D, DFF]), ("ffn1_w_up", [D, DFF]), ("ffn1_w_down", [DFF, D]),
                    ("w_in", [D, NIN]), ("w_branch_a", [1024, D]), ("w_branch_b", [1024, D]),
                    ("w_out", [D, D]), ("ffn2_w_gate", [D, DFF]), ("ffn2_w_up", [D, DFF]),
                    ("ffn2_w_down", [DFF, D]), ("w_ple_gate", [D, D]), ("w_ple_proj", [256, D])):
        W[nm] = din(nm, shp)
    V = {}
    for nm, n in (("ffn1_norm", D), ("mix_norm", D), ("ffn2_norm", D), ("ple_gate_norm", D),
                  ("ple_post_norm", D), ("q_norm", 64), ("k_norm", 64), ("lambda_q1", 64),
                  ("lambda_k1", 64), ("lambda_q2", 64), ("lambda_k2", 64), ("diff_subln", 128),
                  ("hgrn_norm", 128)):
        V[nm] = din(nm, [1, n])
    rel_bias = din("rel_bias", [32, 8])
    lb_logits = din("hgrn_lb_logits", [2, 1024])
    c_ident = din("c_ident", [128, 128], BF16)
    c_blk64 = din("c_blk64", [128, 128], BF16)
    c_mask2 = din("c_mask2", [128, 128], F32)
    c_scan = din("c_scan", [1, S], F32)
    c_oh = din("c_oh", [33, BVL], F32)

    x1 = dscr("x1", [S, D], F32, dbg=True) if stage > 1 else out
    x2 = dscr("x2", [S, D], F32, dbg=True) if stage > 2 else out
    x3 = dscr("x3", [S, D], F32, dbg=True) if stage > 3 else out
    gT = dscr("gT", [16, 128, 44, 128], BF16)
    mT = dscr("mT", [16, 128, 16, 128], BF16)
    yaT = dscr("yaT", [8, 128, S], BF16, dbg=True)
    ybT = dscr("ybT", [8, 128, S], BF16, dbg=True)
    BV = dscr("BV", [8, 128, BVL], F32)
    ple_d = dscr("ple", [S, D], F32)

    K = Sched(nc, es)
    top = Scope(K)
    es.enter_context(top.es)

    ident, ident_r = top.sbr("ident", [128, 128], BF16)
    K.dma("sp", "c_id", ident[:], c_ident, writes=[ident_r])

    with K.scope() as s0:
        rbx, rbx_r = s0.sbr("rbx", [33, 8], F32)
        K.op("dve", I_memset(rbx[:], -30000.0), writes=[rbx_r])
        K.dma("sp", "rbx", rbx[0:32, :], rel_bias, writes=[rbx_r])
        BV_res = [top.res() for _ in range(8)]
        with K.scope() as sbv:
            oh, oh_r = sbv.sbr("oh", [33, BVL], F32)
            K.dma("sp", "oh", oh[:], c_oh, writes=[oh_r])
            rbhs = [sbv.sbr("rbh", [33, 128], F32) for _ in range(2)]
            bvss = [sbv.sbr("bvs", [128, BVL], F32) for _ in range(2)]
            for h in range(8):
                rbh, rbh_r = rbhs[h % 2]
                bvs, bvs_r = bvss[h % 2]
                K.op("dve", I_copy(rbh[:], rbx[:, h:h + 1].to_broadcast([33, 128])), reads=[rbx_r],
                     writes=[rbh_r])
                for c3 in range(3):
                    bank, bres = K.bank()
                    K.op("pe", I_mm([(bank[:, 0:384], rbh[:], oh[:, c3 * 384:(c3 + 1) * 384], True, True)]),
                         reads=[rbh_r, oh_r], writes=[bres])
                    copy_on(K, "dve", bvs[:, c3 * 384:(c3 + 1) * 384], bank[:, 0:384], [bres],
                            [bvs_r] if c3 == 0 else [], merge=[] if c3 == 0 else [bvs_r])
                K.dma("sp", f"bvst{h % 2}", BV[h], bvs[:], reads=[bvs_r], writes=[BV_res[h]])


    def tok_rows(ap, tt):
        return ap[tt * 128:(tt + 1) * 128, :]

    def norm_T(src, src_res, gain, hT, hT_res, nm):
        with K.scope() as sc:
            gb, gb_r = sc.sbr("gb", [128, D], F32)
            xin = [sc.sbr("xin", [128, D], F32) for _ in range(3)]
            hn = [sc.sbr("hn", [128, D], BF16) for _ in range(3)]
            junk, junk_r = sc.sbr("junk", [128, D], BF16)
            ss = [sc.sbr("ss", [128, 1], F32) for _ in range(3)]
            rs = [sc.sbr("rs", [128, 1], F32) for _ in range(3)]
            K.dma("sp", nm + "gb", gb[:], gain[0].partition_broadcast(128), writes=[gb_r])
            K.op("dve", I_ts(gb[:], gb[:], math.sqrt(D), None, ALU.mult), reads=[gb_r], writes=[gb_r])
            def stageA(tt):
                b = tt % 3
                xt, xr = xin[b]
                K.dma("sp", f"{nm}xin{b}", xt[:], tok_rows(src, tt),
                      reads=[src_res[tt]] if src_res else [], writes=[xr])
                K.op("act", I_act(junk[:], xt[:], AF.Square, accum=ss[b][0][:]), reads=[xr],
                     writes=[junk_r, ss[b][1]])
                rsqrt_act(K, rs[b][0][:], ss[b][0][:], D * EPS, [ss[b][1]], rs[b][1])
                K.op("dve", I_stt(hn[b][0][:], xt[:], rs[b][0][:], gb[:], ALU.mult, ALU.mult),
                     reads=[xr, rs[b][1], gb_r], writes=[hn[b][1]])

            stageA(0)
            for tt in range(16):
                b = tt % 3
                if tt + 1 < 16:
                    stageA(tt + 1)
                for half in range(2):
                    bank, bres = K.bank()
                    pb = bank[:].bitcast(BF16)
                    K.op("pe", I_tr([(pb[:, j * 128:(j + 1) * 128],
                                      hn[b][0][:, (half * 8 + j) * 128:(half * 8 + j + 1) * 128])
                                     for j in range(8)], ident[:]),
                         reads=[hn[b][1], ident_r], writes=[bres])
                    copy_on(K, "act" if half == 0 else "dve",
                            hT[:, half * 8:(half + 1) * 8, tt * 128:(tt + 1) * 128],
                            pb.rearrange("p (j c) -> p j c", j=8), [bres], [], merge=[hT_res[tt]])

    def tm_loadw(Wap, KC, nm, t, rl, fb):
        kch = [(k0, min(KC, k0 + 11)) for k0 in range(0, KC, 11)]
        for ci, (k0, k1) in enumerate(kch):
            K.dma("pool", f"{nm}wd{fb % 2}_{ci}", t[:, k0:k1, :],
                  wview(Wap)[:, k0:k1, fb * 512:(fb + 1) * 512], writes=[rl[ci]])

    def gemm_tm(Wap, KC, lhs_get, pre, epilogue, nm, nfb=4, wd0=None):
        with K.scope() as sc:
            kch = [(k0, min(KC, k0 + 11)) for k0 in range(0, KC, 11)]
            wd = [wd0] if wd0 is not None else []
            while len(wd) < 2:
                t = sc.sb("wd", [128, KC, 512], BF16)
                wd.append((t, [sc.res() for _ in kch]))

            def loadw(fb):
                t, rl = wd[fb % 2]
                tm_loadw(Wap, KC, nm, t, rl, fb)

            seq = [(fb, tt) for fb in range(nfb) for tt in range(16)]
            if wd0 is None:
                loadw(0)
            for i in range(min(2, len(seq))):
                pre(sc, i, *seq[i])
            for i, (fb, tt) in enumerate(seq):
                if tt == 0 and fb + 1 < nfb:
                    loadw(fb + 1)
                if i + 2 < len(seq):
                    pre(sc, i + 2, *seq[i + 2])
                lt, lres = lhs_get(i, fb, tt)
                wt, wrl = wd[fb % 2]
                bank, bres = K.bank()
                mm = [(bank[:], lt[:, kc, :], wt[:, kc, :], kc == 0, kc == KC - 1) for kc in range(KC)]
                for ci, (k0, k1) in enumerate(kch):
                    K.op("pe", I_mm(mm[k0:k1]), reads=lres + [wrl[ci]], writes=[bres] if ci == 0 else [],
                         merge=[] if ci == 0 else [bres])
                epilogue(sc, i, fb, tt, bank, bres)

    class TileRing:
        def __init__(self, sc, n, shape, dt, nm):
            self.s = [sc.sbr(nm, shape, dt) for _ in range(n)]
            self.n = n
            self.nm = nm

        def __call__(self, i):
            t, r = self.s[i % self.n]
            return t, r, f"{self.nm}{i % self.n}"

    def ffn(xs, xs_res, xd, xd_res, gain, wg, wu, wdn, nm, hook=None):
        with K.scope() as sc:
            hT = sc.sb("hT", [128, 16, S], BF16)
            hT_res = [sc.res() for _ in range(16)]
            norm_T(xs, xs_res, gain, hT, hT_res, nm + "n")
            gT_res = [sc.res() for _ in range(16)]
            wd0 = (sc.sb("wd0", [128, 44, 512], BF16), [sc.res() for _ in range(4)])
            with K.scope() as sb:
                wr = WRing(K, sb, 8, nm)
                s32 = [sb.sbr("s32", [128, 512], F32) for _ in range(2)]
                grow = [sb.sbr("grow", [128, S], BF16) for _ in range(2)]
                groups = [[(wg, j * 128, 16, hT, lambda t4: hT_res[4 * t4:4 * t4 + 4]),
                           (wu, j * 128, 16, hT, lambda t4: hT_res[4 * t4:4 * t4 + 4])] for j in range(44)]
                cnt = [0]

                def epi(j, t4, ps):
                    (bg, rg), (bu, ru) = ps
                    b = cnt[0] % 2
                    cnt[0] += 1
                    st, sr = s32[b]
                    gt, gr = grow[j % 2]
                    if j == 34 and t4 == 0:
                        tm_loadw(wdn, 44, nm + "d", wd0[0], wd0[1], 0)
                    K.op("act", I_act(st[:], bg[:], AF.Silu), reads=[rg], writes=[sr])
                    K.op("dve", I_tt(gt[:, t4 * 512:(t4 + 1) * 512], st[:], bu[:], ALU.mult),
                         reads=[sr, ru], writes=[], merge=[gr])
                    if t4 == 3:
                        K.dma("sp", f"{nm}grow{j % 2}", gT[:, :, j, :].rearrange("t p c -> p t c"),
                              gt[:].rearrange("p (t c) -> p t c", c=128), reads=[gr], merge=gT_res)

                interleave(gemm_fm_gen(K, wr, groups, epi), hook(sb) if hook else None, 1)
            rings = {}

            def pre(sc2, i, fb, tt):
                if "g" not in rings:
                    rings["g"] = TileRing(sc2, 3, [128, 44, 128], BF16, nm + "gt")
                    rings["x"] = TileRing(sc2, 3, [128, 512], F32, nm + "xr")
                    rings["o"] = TileRing(sc2, 3, [128, 512], F32, nm + "xo")
                t, r, key = rings["g"](i)
                K.dma("sp", key, t[:], gT[tt], reads=[gT_res[tt]], writes=[r])
                t, r, key = rings["x"](i)
                K.dma("sp", key, t[:], xs[tt * 128:(tt + 1) * 128, fb * 512:(fb + 1) * 512],
                      reads=[xs_res[tt]] if xs_res else [], writes=[r])

            def lhs_get(i, fb, tt):
                t, r, _ = rings["g"](i)
                return t, [r]

            def epi2(sc2, i, fb, tt, bank, bres):
                xt, xr, _ = rings["x"](i)
                ot, orr, key = rings["o"](i)
                K.op("dve", I_stt(ot[:], bank[:], 0.5, xt[:], ALU.mult, ALU.add), reads=[bres, xr], writes=[orr])
                K.dma("sp", key, xd[tt * 128:(tt + 1) * 128, fb * 512:(fb + 1) * 512], ot[:],
                      reads=[orr], merge=[xd_res[tt]])

            gemm_tm(wdn, 44, lhs_get, pre, epi2, nm + "d", wd0=wd0)

    def fm_rownorm(src, src_r, gain, gain_r, mulrow, mul_r, dst, dst_r, sq, sq_r, rstd, rstd_r,
                   tmp, tmp_r, ones, ones_r, n):
        K.op("act", I_act(sq[:], src[:], AF.Square), reads=[src_r], writes=[sq_r])
        for t4 in range(4):
            sl = slice(t4 * 512, (t4 + 1) * 512)
            bank, bres = K.bank()
            K.op("pe", I_mm([(bank[:], ones[:], sq[:, sl], True, True)]), reads=[ones_r, sq_r], writes=[bres])
            K.op("act", I_act(rstd[:, sl], bank[:], AF.Ln, bias=float(n * EPS)), reads=[bres], writes=[],
                 merge=[rstd_r])
        K.op("act", I_act(rstd[:], rstd[:], AF.Exp, scale=-0.5), reads=[rstd_r], writes=[rstd_r])
        if mulrow is None:
            K.op("dve", I_stt(dst[:], src[:], gain, rstd[:], ALU.mult, ALU.mult),
                 reads=[src_r, gain_r, rstd_r], writes=[dst_r])
        else:
            K.op("dve", I_stt(tmp[:], src[:], gain, rstd[:], ALU.mult, ALU.mult),
                 reads=[src_r, gain_r, rstd_r], writes=[tmp_r])
            K.op("dve", I_tt(dst[:], tmp[:], mulrow[:], ALU.mult), reads=[tmp_r, mul_r], writes=[dst_r])

    def transpose_row(row, row_r, dst, dst_r, bankfn=None):
        for half in range(2):
            bank, bres = (bankfn or K.bank)()
            pb = bank[:].bitcast(BF16)
            K.op("pe", I_tr([(pb[:, j * 128:(j + 1) * 128],
                              row[:, (half * 8 + j) * 128:(half * 8 + j + 1) * 128]) for j in range(8)],
                            ident[:]), reads=[row_r, ident_r], writes=[bres])
            copy_on(K, "act" if half == 0 else "dve", dst[:, half * 8:(half + 1) * 8, :],
                    pb.rearrange("p (j c) -> p j c", j=8), [bres], [], merge=[dst_r])

    def colvec(sc, ap1n, n, nm, mul=None, reps=1):
        t, r = sc.sbr(nm, [n * reps, 1], F32)
        for i in range(reps):
            K.dma("sp", nm, t[i * n:(i + 1) * n, :], ap1n.rearrange("o n -> n o"), merge=[r])
        if mul is not None:
            K.op("dve", I_ts(t[:], t[:], float(mul), None, ALU.mult), reads=[r], writes=[r])
        return t, r

    def mixer(xs, xs_res, xd, xd_res):
        w_in = W["w_in"]
        mT_res = [top.res() for _ in range(16)]
        with K.scope() as sc:
            hT = sc.sb("h2T", [128, 16, S], BF16)
            hT_res = [sc.res() for _ in range(16)]
            norm_T(xs, xs_res, V["mix_norm"], hT, hT_res, "mn")
            hres4 = lambda t4: hT_res[4 * t4:4 * t4 + 4]
            ya_res = [sc.res() for _ in range(8)]
            yb_res = [sc.res() for _ in range(8)]
            ones, ones_r = sc.sbr("ones", [128, 128], BF16)
            K.op("dve", I_memset(ones[:], 1.0), writes=[ones_r])
            blk, blk_r = sc.sbr("blk", [128, 128], BF16)
            K.dma("sp", "c_blk", blk[:], c_blk64, writes=[blk_r])

            with K.scope() as sa:
                wr = WRing(K, sa, 6, "at")
                qg8, qg8_r = colvec(sa, V["q_norm"], 64, "qg8", 8.0, reps=2)
                kg8, kg8_r = colvec(sa, V["k_norm"], 64, "kg8", 8.0, reps=2)
                sg, sg_r = colvec(sa, V["diff_subln"], 128, "sg", math.sqrt(128.0) * (1.0 - LAMBDA_INIT))
                cball, cball_r = sa.sbr("cball", [128, 8], F32)
                K.dma("sp", "cball", cball[:], rel_bias[31].partition_broadcast(128), writes=[cball_r])
                lv = []
                for nm in ("lambda_q1", "lambda_k1", "lambda_q2", "lambda_k2"):
                    t, r = sa.sbr(nm, [128, 64], F32)
                    K.dma("sp", nm, t[:], V[nm][0].partition_broadcast(128), writes=[r])
                    lv.append((t, r))
                e12 = []
                for a, bb in ((0, 1), (2, 3)):
                    pr, pr_r = sa.sbr("lpr", [128, 64], F32)
                    sm, sm_r = sa.sbr("lsm", [128, 1], F32)
                    K.op("dve", I_tt(pr[:], lv[a][0][:], lv[bb][0][:], ALU.mult), reads=[lv[a][1], lv[bb][1]],
                         writes=[pr_r])
                    K.op("dve", (lambda pr=pr, sm=sm: lambda eng: eng.reduce_sum(out=sm[:], in_=pr[:], axis=AX.X))(),
                         reads=[pr_r], writes=[sm_r])
                    K.op("act", I_act(sm[:], sm[:], AF.Exp), reads=[sm_r], writes=[sm_r])
                    e12.append((sm, sm_r))
                neglam, neglam_r = sa.sbr("neglam", [128, 1], F32)
                K.op("dve", I_tt(neglam[:], e12[1][0][:], e12[0][0][:], ALU.subtract),
                     reads=[e12[0][1], e12[1][1]], writes=[neglam_r])
                K.op("dve", I_ts(neglam[:], neglam[:], -LAMBDA_INIT, None, ALU.add), reads=[neglam_r],
                     writes=[neglam_r])
                if SUB <= 1:
                    return
                q32, q32_r = sa.sbr("q32", [128, S], F32)
                rstd, rstd_r = sa.sbr("rstd", [128, S], F32)
                rstd2, rstd2_r = sa.sbr("rstd2", [128, S], F32)
                sq, sq_r = sa.sbr("sq", [128, S], BF16)
                yrow, yrow_r = sa.sbr("yrow", [128, S], BF16)
                vT, vT_r = sa.sbr("vT", [128, S], BF16)
                hb = [dict(qn=sa.sbr("qn", [128, S], BF16), kn=sa.sbr("kn", [128, S], BF16),
                           Vh=sa.sbr("Vh", [128, 16, 128], BF16)) for _ in range(2)]
                o32, o32_r = sa.sbr("o32", [128, S], F32)
                TTs = [sa.sbr("TT", [128, 1024], F32) for _ in range(2)]
                TT8s = [sa.sbr("TT8", [128, 1024], BF16) for _ in range(2)]
                zcol, zcol_r = sa.sbr("zcol", [128, 1], F32)
                K.op("dve", I_memset(zcol[:], 0.0), writes=[zcol_r])
                Ps = [sa.sbr("P", [128, 2, 512], BF16) for _ in range(3)]
                ep = [sa.sbr("ep", [128, 512], F32) for _ in range(4)]
                cn = {"s": 0, "p": 0, "sb": 0, "pb": 0}
                sq4 = [sa.res() for _ in range(4)]
                q324 = [sa.res() for _ in range(4)]
                rstd4 = [sa.res() for _ in range(4)]
                Laccs = [sa.sbr("Lacc", [128, 2, 512], F32) for _ in range(2)]
                LaccB, LaccB_r = sa.sbr("LaccB", [128, 2, 512], BF16)
                Lacc2_r = [[sa.res() for _ in range(2)] for _ in range(2)]

                def proj_bank():
                    b = K.banks[6 + cn["pb"] % 2]
                    cn["pb"] += 1
                    return b

                def att_proj(h):
                    qn, qn_r = hb[h % 2]["qn"]
                    kn, kn_r = hb[h % 2]["kn"]
                    Vh, Vh_r = hb[h % 2]["Vh"]
                    TT, TT_r = TTs[h % 2]
                    K.dma("sp", f"TT{h % 2}", TT[:],
                          bass.AP(BV.tensor, h * 128 * BVL + 127, [[BVL - 1, 128], [1, 1024]]),
                          reads=[BV_res[h]], writes=[TT_r])
                    TT8, TT8_r = TT8s[h % 2]
                    K.op("dve", I_ts(TT8[:], TT[:], 8.0, None, ALU.mult), reads=[TT_r], writes=[TT8_r])
                    groups = [[(w_in, h * 128, 16, hT, hres4)], [(w_in, 1024 + h * 128, 16, hT, hres4)],
                              [(w_in, 2048 + h * 128, 16, hT, hres4)]]
                    pendB = []

                    def epi(g, t4, ps):
                        (bank, bres), = ps
                        sl = slice(t4 * 512, (t4 + 1) * 512)
                        if g == 2:
                            while pendB:
                                pendB.pop(0)()
                            copy_on(K, "act", vT[:, sl], bank[:], [bres], [], merge=[vT_r])
                            if t4 == 3:
                                transpose_row(vT, vT_r, Vh, Vh_r)
                            return
                        dst, dst_r, gn, gn_r = (qn, qn_r, qg8, qg8_r) if g == 0 else (kn, kn_r, kg8, kg8_r)
                        K.op("act", I_act(sq[:, sl], bank[:], AF.Square), reads=[bres], writes=[sq4[t4]])
                        K.op("dve", I_copy(q32[:, sl], bank[:]), reads=[bres], writes=[q324[t4]])

                        def stageB(sl=sl, dst=dst, dst_r=dst_r, gn=gn, gn_r=gn_r, t4=t4):
                            b2, b2r = K.bank()
                            K.op("pe", I_mm([(b2[:], blk[:], sq[:, sl], True, True)]), reads=[blk_r, sq4[t4]],
                                 writes=[b2r])
                            K.op("act", I_act(rstd[:, sl], b2[:], AF.Ln, bias=float(64 * EPS)), reads=[b2r],
                                 writes=[rstd4[t4]])
                            K.op("act", I_act(rstd[:, sl], rstd[:, sl], AF.Exp, scale=-0.5), reads=[rstd4[t4]],
                                 writes=[rstd4[t4]])
                            K.op("dve", I_stt(dst[:, sl], q32[:, sl], gn[:], rstd[:, sl], ALU.mult, ALU.mult),
                                 reads=[q324[t4], gn_r, rstd4[t4]], writes=[], merge=[dst_r])

                        while pendB:
                            pendB.pop(0)()
                        pendB.append(stageB)

                    yield from gemm_fm_gen(K, wr, groups, epi, prime=True)
                    while pendB:
                        pendB.pop(0)()
                    yield

                def bc2(ap, n):
                    return bass.AP(ap.tensor, ap.offset, [list(ap.ap[0]), [0, 2], list(ap.ap[1])])

                def att_loop(h):
                    qn, qn_r = hb[h % 2]["qn"]
                    kn, kn_r = hb[h % 2]["kn"]
                    Vh, Vh_r = hb[h % 2]["Vh"]
                    TT8, TT8_r = TT8s[h % 2]
                    acc = [K.banks[4], K.banks[6], K.banks[5], K.banks[7]]
                    for qt in range(4):
                        nkb = 4 * qt + 4
                        Lacc, Lacc_r = Laccs[qt % 2]
                        Sb = {}

                        def emit_S(kb, qt=qt):
                            c0 = max(0, 128 * (kb - 4 * qt))
                            p = cn["sb"] % 2
                            cn["sb"] += 1
                            near = (kb - 4 * qt) >= -1
                            o0 = 384 - 128 * (kb - 4 * qt)
                            for c in range(2):
                                bS, bSr = K.banks[2 * p + c]
                                mm = [(bS[:, c0:512], kn[c * 64:(c + 1) * 64, kb * 128:(kb + 1) * 128],
                                       qn[c * 64:(c + 1) * 64, qt * 512 + c0:(qt + 1) * 512], True, not near)]
                                if near:
                                    mm.append((bS[:, c0:512], ident[:], TT8[:, o0 + c0:o0 + 512], False, True))
                                K.op("pe", I_mm(mm), reads=[kn_r, qn_r] + ([ident_r, TT8_r] if near else []),
                                     writes=[bSr])
                            Sb[kb] = p

                        emit_S(0)
                        for kb in range(nkb):
                            if kb + 1 < nkb:
                                emit_S(kb + 1)
                            i = kb - 4 * qt
                            c0 = max(0, 128 * i)
                            p = Sb.pop(kb)
                            bres2 = [K.banks[2 * p][1], K.banks[2 * p + 1][1]]
                            S2 = K.ps[:, 2 * p:2 * p + 2, c0:512]
                            P, P_r = Ps[cn["p"] % 3]
                            cn["p"] += 1
                            if i <= -2:
                                K.op("act", I_act(P[:, :, c0:512], S2, AF.Exp, bias=cball[:, h:h + 1],
                                                  scale=0.125), reads=bres2 + [cball_r], writes=[P_r])
                            else:
                                K.op("act", I_act(P[:, :, c0:512], S2, AF.Exp, bias=zcol[:], scale=0.125),
                                     reads=bres2 + [zcol_r], writes=[P_r])
                            for c in range(2):
                                (bO, bOr) = acc[2 * c]
                                K.op("pe", I_mm([(bO[:, c0:512], Vh[:, kb, :], P[:, c, c0:512], kb == 0,
                                                  kb == nkb - 1)]),
                                     reads=[Vh_r, P_r], writes=[] if kb else [bOr], merge=[bOr] if kb else [])
                            for c, eng_ in ((0, "dve"), (1, "pool")):
                                if kb == 0:
                                    K.op(eng_, I_copy(Lacc[:, c, :], P[:, c, :]), reads=[P_r], writes=[Lacc2_r[qt % 2][c]])
                                else:
                                    K.op(eng_, I_tt(Lacc[:, c, c0:512], Lacc[:, c, c0:512], P[:, c, c0:512], ALU.add),
                                         reads=[P_r, Lacc2_r[qt % 2][c]], writes=[Lacc2_r[qt % 2][c]])
                            yield
                        K.op("dve", I_copy(LaccB[:], Lacc[:]), reads=Lacc2_r[qt % 2], writes=[LaccB_r])
                        for c in range(2):
                            bL, bLr = acc[2 * c + 1]
                            K.op("pe", I_mm([(bL[:], ones[:], LaccB[:, c, :], True, True)]), reads=[ones_r, LaccB_r],
                                 writes=[bLr])
                        (bO1, rO1), (bL1, rL1), (bO2, rO2), (bL2, rL2) = acc
                        (r1, r1r), (r2, r2r), (o1, o1r), (t2, t2r) = ep
                        sl = slice(qt * 512, (qt + 1) * 512)
                        K.op("act", I_act(r1[:], bL1[:], AF.Ln), reads=[rL1], writes=[r1r])
                        K.op("act", I_act(r2[:], bL2[:], AF.Ln), reads=[rL2], writes=[r2r])
                        K.op("act", I_act(r1[:], r1[:], AF.Exp, scale=-1.0), reads=[r1r], writes=[r1r])
                        K.op("act", I_act(r2[:], r2[:], AF.Exp, scale=-1.0), reads=[r2r], writes=[r2r])
                        K.op("dve", I_tt(o1[:], bO1[:], r1[:], ALU.mult), reads=[rO1, r1r], writes=[o1r])
                        K.op("dve", I_stt(t2[:], bO2[:], neglam[:], r2[:], ALU.mult, ALU.mult),
                             reads=[rO2, neglam_r, r2r], writes=[t2r])
                        K.op("dve", I_tt(o32[:, sl], o1[:], t2[:], ALU.add), reads=[o1r, t2r], writes=[],
                             merge=[o32_r])
                        yield
                    if SUBA <= 3:
                        return
                    fm_rownorm(o32, o32_r, sg[:], sg_r, None, None, yrow, yrow_r, yrow, yrow_r, rstd2, rstd2_r,
                               None, None, ones, ones_r, 128)
                    K.dma("sp", "yast", yaT[h], yrow[:], reads=[yrow_r], writes=[ya_res[h]])
                    yield

                gp = att_proj(0)
                next(gp)
                for h in range(NH):
                    for _ in gp:
                        pass
                    if h + 1 < NH:
                        gp = att_proj(h + 1)
                        next(gp)
                    for _ in att_loop(h):
                        pass

            if SUB <= 2:
                return
            with K.scope() as sh:
                wr = WRing(K, sh, 6, "hg")
                hgn, hgn_r = colvec(sh, V["hgrn_norm"], 128, "hgn", math.sqrt(128.0))
                lg, lg_r = sh.sbr("lg", [128, 2, 8], F32)
                for r_ in range(2):
                    for h_ in range(8):
                        K.dma("sp", "lg", lg[:, r_, h_:h_ + 1],
                              lb_logits[r_:r_ + 1, h_ * 128:(h_ + 1) * 128].rearrange("o n -> n o"), merge=[lg_r])
                lb, lb_r = sh.sbr("lb", [128, 8], F32)
                oml, oml_r = sh.sbr("oml", [128, 8], F32)
                K.op("dve", I_tt(lb[:], lg[:, 0, :], lg[:, 1, :], ALU.subtract), reads=[lg_r], writes=[lb_r])
                K.op("act", I_act(lb[:], lb[:], AF.Sigmoid), reads=[lb_r], writes=[lb_r])
                K.op("dve", I_ts(oml[:], lb[:], -1.0, 1.0, ALU.mult, ALU.add), reads=[lb_r], writes=[oml_r])
                scm, scm_r = sh.sbr("scm", [128, S], F32)
                K.dma("sp", "scm", scm[:], c_scan[0].partition_broadcast(128), writes=[scm_r])
                mk2, mk2_r = sh.sbr("mk2", [128, 128], F32)
                K.dma("sp", "mk2", mk2[:], c_mask2, writes=[mk2_r])
                hs = [dict(R1=sh.sbr("R1", [128, S], F32),
                           R2=sh.sbr("R2", [128, S], F32),
                           viT=sh.sbr("viT", [128, S], BF16),
                           og=sh.sbr("og", [128, S], BF16),
                           Vh=sh.sbr("Vhh", [128, 16, 128], BF16)) for _ in range(2)]
                R3, R3r = sh.sbr("R3", [128, S], F32)
                R4, R4r = sh.sbr("R4", [128, S], F32)
                R5, R5r = sh.sbr("R5", [128, S], F32)
                Qt, Qt_r = sh.sbr("Qt", [128, S], BF16)
                Kt, Kt_r = sh.sbr("Kt", [128, S], BF16)
                Kh, Kh_r = sh.sbr("Kh", [128, S], BF16)
                KhT, KhT_r = sh.sbr("KhT", [128, 16, 128], BF16)
                st32 = [sh.sbr("st32", [128, 128], F32) for _ in range(2)]
                st16 = [sh.sbr("st16", [128, 128], BF16) for _ in range(6)]
                STm = [sh.sbr("STm", [128, 128], BF16) for _ in range(2)]

                def hg_proj(h):
                    st = hs[h % 2]
                    (R1, R1r), (R2, R2r), (viT, viT_r), (og, og_r), (Vh, Vh_r) = (st["R1"], st["R2"], st["viT"],
                                                                                   st["og"], st["Vh"])
                    order = [2, 0, 1, 3]
                    groups = [[(w_in, (3 + g) * 1024 + h * 128, 16, hT, hres4)] for g in order]

                    def epi(gi, t4, ps):
                        g = order[gi]
                        (bank, bres), = ps
                        sl = slice(t4 * 512, (t4 + 1) * 512)
                        if g == 0:
                            K.op("act", I_act(R1[:, sl], bank[:], AF.Silu), reads=[bres], writes=[], merge=[R1r])
                        elif g == 1:
                            K.op("act", I_act(R2[:, sl], bank[:], AF.Sigmoid), reads=[bres], writes=[], merge=[R2r])
                        elif g == 2:
                            copy_on(K, "dve", viT[:, sl], bank[:], [bres], [], merge=[viT_r])
                            if t4 == 3:
                                transpose_row(viT, viT_r, Vh, Vh_r)
                        else:
                            K.op("act", I_act(og[:, sl], bank[:], AF.Silu), reads=[bres], writes=[], merge=[og_r])

                    yield from gemm_fm_gen(K, wr, groups, epi, split=2)

                def hg_main(h):
                    st = hs[h % 2]
                    (R1, R1r), (R2, R2r), (viT, viT_r), (og, og_r), (Vh, Vh_r) = (st["R1"], st["R2"], st["viT"],
                                                                                   st["og"], st["Vh"])
                    K.op("dve", I_ts(R2[:], R2[:], oml[:, h:h + 1], lb[:, h:h + 1], ALU.mult, ALU.add),
                         reads=[R2r, oml_r, lb_r], writes=[R2r])
                    yield
                    K.op("act", I_act(R3[:], R2[:], AF.Ln), reads=[R2r], writes=[R3r])
                    K.op("dve", I_ts(R2[:], R2[:], -1.0, 1.0, ALU.mult, ALU.add), reads=[R2r], writes=[R2r])
                    yield
                    K.op("dve", lambda eng: eng.tensor_tensor_scan(out=R4[:], data0=scm[:], data1=R3[:], initial=0.0,
                                                                   op0=ALU.mult, op1=ALU.add),
                         reads=[scm_r, R3r], writes=[R4r])
                    K.op("act", I_act(R5[:], R4[:], AF.Exp), reads=[R4r], writes=[R5r])
                    yield
                    K.op("act", I_act(R3[:], R4[:], AF.Exp, scale=-1.0), reads=[R4r], writes=[R3r])
                    K.op("dve", I_tt(Qt[:], R1[:], R5[:], ALU.mult), reads=[R1r, R5r], writes=[Qt_r])
                    yield
                    K.op("dve", I_tt(Kt[:], R2[:], R3[:], ALU.mult), reads=[R2r, R3r], writes=[Kt_r])
                    b3 = R4[:].rearrange("p (c t) -> p c t", t=64)
                    K.op("dve", I_tt(R3[:].rearrange("p (c t) -> p c t", t=64),
                                     b3[:, :, 63:64].to_broadcast([128, 32, 64]), b3, ALU.subtract),
                         reads=[R4r], writes=[R3r])
                    yield
                    K.op("act", I_act(R3[:], R3[:], AF.Exp), reads=[R3r], writes=[R3r])
                    K.op("dve", I_tt(Kh[:], R2[:], R3[:], ALU.mult), reads=[R2r, R3r], writes=[Kh_r])
                    transpose_row(Kh, Kh_r, KhT, KhT_r)
                    yield
                    e3 = R5[:].rearrange("p (c t) -> p c t", t=64)
                    K.op("dve", I_memset(st32[0][0][:], 0.0), writes=[st32[0][1]])
                    K.op("dve", I_memset(st16[0][0][:], 0.0), writes=[st16[0][1]])
                    pre_ = {}

                    def emit_pre(j):
                        tk = slice(j * 128, (j + 1) * 128)
                        bST, bSTr = K.bank()
                        K.op("pe", I_mm([(bST[:, 0:128], Kt[:, tk], Qt[:, tk], True, True)]), reads=[Kt_r, Qt_r],
                             writes=[bSTr])
                        bU = [K.bank(), K.bank()]
                        for u in range(2):
                            K.op("pe", I_mm([(bU[u][0][:, 0:128], KhT[u * 64:(u + 1) * 64, j, :],
                                              Vh[u * 64:(u + 1) * 64, j, :], True, True)]),
                                 reads=[KhT_r, Vh_r], writes=[bU[u][1]])
                        pre_[j] = (bST, bSTr, bU)

                    emit_pre(0)
                    for j in range(16):
                        tk = slice(j * 128, (j + 1) * 128)
                        bST, bSTr, bU = pre_.pop(j)
                        sm, sm_r = STm[j % 2]
                        K.op("dve", I_tt(sm[:], bST[:, 0:128], mk2[:], ALU.mult), reads=[bSTr, mk2_r], writes=[sm_r])
                        s16 = [st16[(2 * j) % 6], st16[(2 * j + 1) % 6], st16[(2 * j + 2) % 6]]
                        s32 = [st32[0], st32[1], st32[0]]
                        for u in range(2):
                            K.op("dve", I_stt(s32[u + 1][0][:], s32[u][0][:], e3[:, 2 * j + u, 63:64], bU[u][0][:, 0:128],
                                              ALU.mult, ALU.add), reads=[s32[u][1], R5r, bU[u][1]], writes=[s32[u + 1][1]])
                            copy_on(K, "act", s16[u + 1][0][:], s32[u + 1][0][:], [s32[u + 1][1]], [s16[u + 1][1]])
                        if j + 1 < 16:
                            emit_pre(j + 1)
                        bO, bOr = K.bank()
                        K.op("pe", I_mm([(bO[:, 0:128], Vh[:, j, :], sm[:], True, False),
                                         (bO[:, 0:64], s16[0][0][:], Qt[:, j * 128:j * 128 + 64], False, False),
                                         (bO[:, 64:128], s16[1][0][:], Qt[:, j * 128 + 64:(j + 1) * 128], False, True)]),
                             reads=[Vh_r, sm_r, s16[0][1], s16[1][1], Qt_r], writes=[bOr])
                        copy_on(K, "act", R1[:, tk], bO[:, 0:128], [bOr], [], merge=[R1r])
                        yield
                    fm_rownorm(R1, R1r, hgn[:], hgn_r, og, og_r, Kh, Kh_r, viT, viT_r, R4, R4r, R3, R3r,
                               ones, ones_r, 128)
                    K.dma("sp", "ybst", ybT[h], Kh[:], reads=[Kh_r], writes=[yb_res[h]])
                    yield

                for _ in hg_proj(0):
                    pass
                for h in range(8):
                    interleave(hg_main(h), hg_proj(h + 1) if h + 1 < 8 else None, 2)

            if SUB <= 3:
                return
            with K.scope() as sm_:
                wr = WRing(K, sm_, 8, "mg")
                yas = sm_.sb("yas", [128, 8, S], BF16)
                ybs = sm_.sb("ybs", [128, 8, S], BF16)
                yas_r = [sm_.res() for _ in range(8)]
                ybs_r = [sm_.res() for _ in range(8)]
                for h in range(8):
                    K.dma("sp", "yald", yas[:, h, :], yaT[h], reads=[ya_res[h]], writes=[yas_r[h]])
                    K.dma("sp", "ybld", ybs[:, h, :], ybT[h], reads=[yb_res[h]], writes=[ybs_r[h]])
                sg_ = [sm_.sbr("sga", [128, 512], F32) for _ in range(2)]
                sb_ = [sm_.sbr("sgb", [128, 512], F32) for _ in range(2)]
                mrow = [sm_.sbr("mrow", [128, S], BF16) for _ in range(2)]
                groups = [[(w_in, 7168 + m * 128, 16, hT, hres4), (w_in, 9216 + m * 128, 16, hT, hres4),
                           (W["w_branch_a"], m * 128, 8, yas, lambda t4: yas_r),
                           (W["w_branch_b"], m * 128, 8, ybs, lambda t4: ybs_r)] for m in range(16)]
                cnt = [0]

                def epi(m, t4, ps):
                    (bga, rga), (bgb, rgb), (bba, rba), (bbb, rbb) = ps
                    b = cnt[0] % 2
                    cnt[0] += 1
                    (ta, tar), (tb, tbr) = sg_[b], sb_[b]
                    mt, mr = mrow[m % 2]
                    sl = slice(t4 * 512, (t4 + 1) * 512)
                    K.op("act", I_act(ta[:], bga[:], AF.Sigmoid), reads=[rga], writes=[tar])
                    K.op("act", I_act(tb[:], bgb[:], AF.Sigmoid), reads=[rgb], writes=[tbr])
                    K.op("dve", I_tt(ta[:], ta[:], bba[:], ALU.mult), reads=[tar, rba], writes=[tar])
                    K.op("dve", I_tt(tb[:], tb[:], bbb[:], ALU.mult), reads=[tbr, rbb], writes=[tbr])
                    K.op("dve", I_tt(mt[:, sl], ta[:], tb[:], ALU.add), reads=[tar, tbr], writes=[], merge=[mr])
                    if t4 == 3:
                        K.dma("sp", f"mrow{m % 2}", mT[:, :, m, :].rearrange("t p c -> p t c"),
                              mt[:].rearrange("p (t c) -> p t c", c=128), reads=[mr], merge=mT_res)

                gemm_fm(K, wr, groups, epi)

        if SUB <= 4:
            return
        rings = {}

        def pre(sc2, i, fb, tt):
            if "g" not in rings:
                rings["g"] = [sc2.sbr("wo_m", [128, 16, 128], BF16) for _ in range(16)]
                for t_ in range(16):
                    K.dma("sp", f"wo_m{t_ % 4}", rings["g"][t_][0][:], mT[t_], reads=[mT_res[t_]],
                          writes=[rings["g"][t_][1]])
                rings["x"] = TileRing(sc2, 3, [128, 512], F32, "wo_xr")
                rings["o"] = TileRing(sc2, 3, [128, 512], F32, "wo_xo")
            t, r, key = rings["x"](i)
            K.dma("sp", key, t[:], xs[tt * 128:(tt + 1) * 128, fb * 512:(fb + 1) * 512],
                  reads=[xs_res[tt]], writes=[r])

        def lhs_get(i, fb, tt):
            t, r = rings["g"][tt]
            return t, [r]

        def epi2(sc2, i, fb, tt, bank, bres):
            xt, xr, _ = rings["x"](i)
            ot, orr, key = rings["o"](i)
            K.op("dve", I_tt(ot[:], bank[:], xt[:], ALU.add), reads=[bres, xr], writes=[orr])
            K.dma("sp", key, xd[tt * 128:(tt + 1) * 128, fb * 512:(fb + 1) * 512], ot[:],
                  reads=[orr], merge=[xd_res[tt]])

        gemm_tm(W["w_out"], 16, lhs_get, pre, epi2, "wo")

    def ple_proj_gen(sc, ple_res):
        if True:
            wp, wp_r = sc.sbr("wp", [128, 2, D], BF16)
            K.dma("pool", "wp", wp[:], wview(W["w_ple_proj"]), writes=[wp_r])
            gb, gb_r = sc.sbr("pgb", [128, D], F32)
            K.dma("sp", "pgb", gb[:], V["ple_post_norm"][0].partition_broadcast(128), writes=[gb_r])
            K.op("dve", I_ts(gb[:], gb[:], math.sqrt(D), None, ALU.mult), reads=[gb_r], writes=[gb_r])
            pin = [sc.sbr("pin", [128, 256], F32) for _ in range(2)]
            pbf = [sc.sbr("pbf", [128, 256], BF16) for _ in range(2)]
            pT = [sc.sbr("pT", [128, 2, 128], BF16) for _ in range(2)]
            pl32 = [sc.sbr("pl32", [128, D], F32) for _ in range(2)]
            junk, junk_r = sc.sbr("pjunk", [128, D], BF16)
            ss = [sc.sbr("pss", [128, 1], F32) for _ in range(2)]
            for tt in range(16):
                b = tt % 2
                K.dma("sp", f"pin{b}", pin[b][0][:], p_in[tt * 128:(tt + 1) * 128, :], writes=[pin[b][1]])
                K.op("dve", I_copy(pbf[b][0][:], pin[b][0][:]), reads=[pin[b][1]], writes=[pbf[b][1]])
                bank, bres = K.bank()
                pb = bank[:].bitcast(BF16)
                K.op("pe", I_tr([(pb[:, j * 128:(j + 1) * 128], pbf[b][0][:, j * 128:(j + 1) * 128])
                                 for j in range(2)], ident[:]), reads=[pbf[b][1], ident_r], writes=[bres])
                copy_on(K, "dve", pT[b][0][:], pb[:, 0:256].rearrange("p (j c) -> p j c", j=2), [bres], [pT[b][1]])
                for fb in range(4):
                    bank, bres = K.bank()
                    K.op("pe", I_mm([(bank[:], pT[b][0][:, kc, :], wp[:, kc, fb * 512:(fb + 1) * 512], kc == 0, kc == 1)
                                     for kc in range(2)]), reads=[pT[b][1], wp_r], writes=[bres])
                    copy_on(K, "act", pl32[b][0][:, fb * 512:(fb + 1) * 512], bank[:], [bres], [], merge=[pl32[b][1]])
                K.op("act", I_act(junk[:], pl32[b][0][:], AF.Square, accum=ss[b][0][:]), reads=[pl32[b][1]],
                     writes=[junk_r, ss[b][1]])
                rsqrt_act(K, ss[b][0][:], ss[b][0][:], D * EPS, [ss[b][1]], ss[b][1])
                K.op("dve", I_stt(pl32[b][0][:], pl32[b][0][:], ss[b][0][:], gb[:], ALU.mult, ALU.mult),
                     reads=[pl32[b][1], ss[b][1], gb_r], writes=[pl32[b][1]])
                K.dma("sp", f"plst{b}", ple_d[tt * 128:(tt + 1) * 128, :], pl32[b][0][:], reads=[pl32[b][1]],
                      writes=[ple_res[tt]])
                yield

    def ple_phase(xs, xs_res, xd, xd_res, ple_res):
        with K.scope() as sc:
            hT = sc.sb("h4T", [128, 16, S], BF16)
            hT_res = [sc.res() for _ in range(16)]
            norm_T(xs, xs_res, V["ple_gate_norm"], hT, hT_res, "pn")
            rings = {}

            def pre(sc2, i, fb, tt):
                if "x" not in rings:
                    rings["x"] = TileRing(sc2, 3, [128, 512], F32, "pg_xr")
                    rings["p"] = TileRing(sc2, 3, [128, 512], F32, "pg_pl")
                    rings["s"] = TileRing(sc2, 3, [128, 512], F32, "pg_sg")
                t, r, key = rings["x"](i)
                K.dma("sp", key, t[:], xs[tt * 128:(tt + 1) * 128, fb * 512:(fb + 1) * 512],
                      reads=[xs_res[tt]], writes=[r])
                t, r, key = rings["p"](i)
                K.dma("sp", key, t[:], ple_d[tt * 128:(tt + 1) * 128, fb * 512:(fb + 1) * 512],
                      reads=[ple_res[tt]], writes=[r])

            def lhs_get(i, fb, tt):
                return hT[:, :, tt * 128:(tt + 1) * 128], [hT_res[tt]]

            def epi2(sc2, i, fb, tt, bank, bres):
                xt, xr, _ = rings["x"](i)
                pt, pr, _ = rings["p"](i)
                st, sr, key = rings["s"](i)
                K.op("act", I_act(st[:], bank[:], AF.Sigmoid), reads=[bres], writes=[sr])
                K.op("dve", I_tt(st[:], st[:], pt[:], ALU.mult), reads=[sr, pr], writes=[sr])
                K.op("dve", I_tt(st[:], st[:], xt[:], ALU.add), reads=[sr, xr], writes=[sr])
                K.dma("sp", key, xd[tt * 128:(tt + 1) * 128, fb * 512:(fb + 1) * 512], st[:],
                      reads=[sr], merge=[xd_res[tt]])

            gemm_tm(W["w_ple_gate"], 16, lhs_get, pre, epi2, "pg")

    x1_res = [top.res() for _ in range(16)]
    x2_res = [top.res() for _ in range(16)]
    x3_res = [top.res() for _ in range(16)]
    out_res = [top.res() for _ in range(16)]

    ffn(x_in, None, x1, x1_res if stage > 1 else out_res, V["ffn1_norm"],
        W["ffn1_w_gate"], W["ffn1_w_up"], W["ffn1_w_down"], "f1")
    if stage > 1:
        mixer(x1, x1_res, x2, x2_res if stage > 2 else out_res)
    ple_res = [top.res() for _ in range(16)]

    def spread(gen, k):
        for _ in gen:
            for _k in range(k):
                yield

    if stage > 2:
        ffn(x2, x2_res, x3, x3_res if stage > 3 else out_res, V["ffn2_norm"],
            W["ffn2_w_gate"], W["ffn2_w_up"], W["ffn2_w_down"], "f2",
            hook=(lambda sc: spread(ple_proj_gen(sc, ple_res), 8)) if stage > 3 else None)
    if stage > 3:
        ple_phase(x3, x3_res, out, out_res, ple_res)

    K.wait_all("sp", out_res)
    K.emit()
    es.close()
    return nc


def _consts():
    bf = ml_dtypes.bfloat16
    ident = np.eye(128, dtype=np.float32).astype(bf)
    blk = np.zeros((128, 128), np.float32)
    blk[:64, :64] = 1
    blk[64:, 64:] = 1
    s = np.arange(128)[:, None]
    t = np.arange(128)[None, :]
    mask2 = ((s // 64 == t // 64) & (t >= s)).astype(np.float32)
    scan = np.ones((1, S), np.float32)
    scan[0, ::64] = 0
    oh = np.zeros((33, BVL), np.float32)
    for idx in range(BVL - 1):
        dist = idx - 511
        if dist < 0:
            oh[32, idx] = 1
        else:
            n = dist
            if n < 16:
                b = n
            else:
                nf = np.float32(max(n, 1))
                b = 16 + int(np.float32(np.log(nf / np.float32(16)) / np.float32(math.log(8.0))
                                        * np.float32(16)))
                b = min(b, 31)
            oh[b, idx] = 1
    return dict(c_ident=ident, c_blk64=blk.astype(bf), c_mask2=mask2, c_scan=scan, c_oh=oh)


_W2 = ("ffn1_w_gate", "ffn1_w_up", "ffn1_w_down", "w_in", "w_branch_a", "w_branch_b", "w_out",
       "ffn2_w_gate", "ffn2_w_up", "ffn2_w_down", "w_ple_gate", "w_ple_proj")
_V1 = ("ffn1_norm", "mix_norm", "ffn2_norm", "ple_gate_norm", "ple_post_norm", "q_norm", "k_norm",
       "lambda_q1", "lambda_k1", "lambda_q2", "lambda_k2", "diff_subln", "hgrn_norm")


def kernel(**inputs):
    stage = int(os.environ.get("MK_STAGE", "99"))
    debug = bool(int(os.environ.get("MK_DEBUG", "0")))
    nc = build(stage, debug)
    shared = {}
    for k in _W2:
        shared[k] = np.ascontiguousarray(np.asarray(inputs[k], np.float32)[0])
    for k in _V1:
        shared[k] = np.ascontiguousarray(np.asarray(inputs[k], np.float32))
    shared["rel_bias"] = np.ascontiguousarray(np.asarray(inputs["rel_bias"], np.float32))
    shared["hgrn_lb_logits"] = np.ascontiguousarray(np.asarray(inputs["hgrn_lb_logits"], np.float32))
    shared.update(_consts())
    x = np.asarray(inputs["x"], np.float32)
    p = np.asarray(inputs["p"], np.float32)
    in_maps = []
    ncores = int(os.environ.get("MK_CORES", NCORES))
    for c in range(ncores):
        m = dict(shared)
        m["x"] = np.ascontiguousarray(x[c])
        m["p"] = np.ascontiguousarray(p[0, c])
        in_maps.append(m)
    res = run_bass_kernel_spmd(nc, in_maps, core_ids=list(range(ncores)))
    if debug:
        kernel.last = res.results
    return np.stack([np.asarray(r["out"], np.float32) for r in res.results], axis=0)
```

```python
import os
import math
from contextlib import ExitStack, contextmanager
import numpy as np
import ml_dtypes
import concourse.bass as bass
import concourse.mybir as mybir
from concourse.bass_utils import run_bass_kernel_spmd

F32 = mybir.dt.float32
BF16 = mybir.dt.bfloat16
AF = mybir.ActivationFunctionType
ALU = mybir.AluOpType
AX = mybir.AxisListType

S = 2048
D = 2048
DFF = 5632
NIN = 11264
EPS = 1e-6
NCORES = 8
LAMBDA_INIT = 0.8 - 0.6 * math.exp(-0.3 * 0)
BVL = 1152
SUB = int(os.environ.get("MK_SUB", "9"))
SUBA = int(os.environ.get("MK_SUBA", "9"))
NH = int(os.environ.get("MK_NH", "8"))


class Res:
    __slots__ = ("w", "r", "isbank")

    def __init__(self, K, isbank=False):
        self.w = dict(K.grave)
        self.r = {}
        self.isbank = isbank


class Scope:
    def __init__(self, K):
        self.K = K
        self.es = ExitStack()
        self.rs = []

    def sb(self, name, shape, dt):
        self.K.uid += 1
        return self.es.enter_context(self.K.nc.sbuf_tensor(f"{name}_{self.K.uid}", shape, dt))

    def res(self):
        r = Res(self.K)
        self.rs.append(r)
        return r

    def sbr(self, name, shape, dt):
        return self.sb(name, shape, dt), self.res()


class Sched:
    def __init__(self, nc, es):
        self.nc = nc
        self.es = es
        self.engs = dict(pe=nc.tensor, act=nc.scalar, dve=nc.vector, pool=nc.gpsimd, sp=nc.sync)
        self.sem = {}
        self.cnt = {}
        self.prog = {}
        self.waited = {}
        self.grave = {}
        self.uid = 0
        for e in self.engs:
            self.sem["s_" + e] = es.enter_context(nc.semaphore("s_" + e))
            self.cnt["s_" + e] = 0
            self.prog[e] = []
            self.waited[e] = {}
        self.nphys = {"pool": 26, "sp": 64}
        self.k2p = {"pool": {}, "sp": {}}
        for q, n in self.nphys.items():
            for i in range(n):
                self.sem[f"{q}{i}"] = es.enter_context(nc.semaphore(f"{q}{i}"))
                self.cnt[f"{q}{i}"] = 0
        self.banks = []
        self.ps = es.enter_context(nc.psum_tensor("psall", [128, 8, 512], F32))
        for i in range(8):
            self.banks.append((self.ps[:, i, :], Res(self, True)))
        self.bi = 0

    def bank(self):
        b = self.banks[self.bi]
        self.bi = (self.bi + 1) % 8
        return b

    @contextmanager
    def scope(self):
        sc = Scope(self)
        try:
            yield sc
        finally:
            for r in sc.rs:
                for d in (r.w, r.r):
                    for k, v in d.items():
                        if self.grave.get(k, 0) < v:
                            self.grave[k] = v
            sc.es.close()

    def _need(self, e, reads, writes, merge):
        need = {}
        for r in reads:
            for k, v in r.w.items():
                if need.get(k, 0) < v:
                    need[k] = v
        for w in writes:
            for d in (w.w, w.r):
                for k, v in d.items():
                    if need.get(k, 0) < v:
                        need[k] = v
        for w in merge:
            for k, v in w.r.items():
                if need.get(k, 0) < v:
                    need[k] = v
        if e == "pe":
            need.pop("s_pe", None)
        wd = self.waited[e]
        waits = []
        for k, v in need.items():
            if wd.get(k, 0) < v:
                wd[k] = v
                waits.append((k, v))
        return waits

    def _mark(self, tok, reads, writes, merge):
        k, v = tok
        for r in reads:
            r.r[k] = v
        for w in writes:
            w.w = {k: v}
            w.r = {}
        for w in merge:
            w.w[k] = v

    def op(self, e, fn, reads=(), writes=(), merge=()):
        if e != "pe":
            br = [r for r in reads if r.isbank]
            if br:
                reads = [r for r in reads if not r.isbank]
                writes = list(writes) + br
        waits = self._need(e, reads, writes, merge)
        key = "s_" + e
        self.cnt[key] += 1
        tok = (key, self.cnt[key])
        self.prog[e].append((waits, fn, tok, 1))
        self._mark(tok, reads, writes, merge)

    def dma(self, q, key, out, in_, reads=(), writes=(), merge=(), slow=False):
        k2p = self.k2p[q]
        if key not in k2p:
            k2p[key] = f"{q}{len(k2p) % self.nphys[q]}"
        key = k2p[key]
        waits = self._need(q, reads, writes, merge)
        c = self.cnt[key]
        if c and self.waited[q].get(key, 0) < c:
            self.waited[q][key] = c
            waits.append((key, c))
        self.cnt[key] += 16
        tok = (key, self.cnt[key])
        if slow:
            fn = lambda eng: eng.dma_start(out=out, in_=in_, allow_slow_non_contiguous=True)
        else:
            fn = lambda eng: eng.dma_start(out=out, in_=in_)
        self.prog[q].append((waits, fn, tok, 16))
        self._mark(tok, reads, writes, merge)

    def wait_all(self, e, rs):
        waits = self._need(e, rs, (), ())
        self.prog[e].append((waits, None, None, 0))

    def emit(self):
        nc = self.nc
        with nc.Block() as block:
            for e, deco in (("sp", block.sync), ("act", block.scalar), ("dve", block.vector),
                            ("pool", block.gpsimd), ("pe", block.tensor)):
                prog = self.prog[e]

                def body(eng, prog=prog):
                    for waits, fn, tok, inc in prog:
                        for k, v in waits:
                            eng.wait_ge(self.sem[k], v)
                        if fn is not None:
                            fn(eng).then_inc(self.sem[tok[0]], inc)

                deco(body)


def I_act(out, in_, func, bias=None, scale=None, accum=None):
    def f(eng):
        kw = {}
        if bias is not None:
            kw["bias"] = bias
        if scale is not None:
            kw["scale"] = scale
        if accum is not None:
            kw["accum_out"] = accum
        return eng.activation(out=out, in_=in_, func=func, **kw)
    return f


def I_stt(out, in0, scalar, in1, op0, op1):
    return lambda eng: eng.scalar_tensor_tensor(out=out, in0=in0, scalar=scalar, in1=in1, op0=op0, op1=op1)


def I_ts(out, in0, s1, s2, op0, op1=None):
    if op1 is None:
        return lambda eng: eng.tensor_scalar(out=out, in0=in0, scalar1=s1, scalar2=None, op0=op0)
    return lambda eng: eng.tensor_scalar(out=out, in0=in0, scalar1=s1, scalar2=s2, op0=op0, op1=op1)


def I_tt(out, in0, in1, op):
    return lambda eng: eng.tensor_tensor(out=out, in0=in0, in1=in1, op=op)


def I_copy(out, in_):
    return lambda eng: eng.tensor_copy(out=out, in_=in_)


def I_acopy(out, in_):
    return lambda eng: eng.activation(out=out, in_=in_, func=AF.Copy)


def I_mm(lst):
    def f(eng):
        ins = None
        for (o, l, r, st, sp) in lst:
            ins = eng.matmul(o, lhsT=l, rhs=r, start=st, stop=sp)
        return ins
    return f


def I_tr(lst, ident):
    def f(eng):
        ins = None
        for (o, i) in lst:
            ins = eng.transpose(out=o, in_=i, identity=ident)
        return ins
    return f


def I_memset(ap, v):
    return lambda eng: eng.memset(ap, v)


def rsqrt_act(K, out, in_, add, reads, wres):
    K.op("act", I_act(out, in_, AF.Ln, bias=add), reads=reads, writes=[wres])
    K.op("act", I_act(out, out, AF.Exp, scale=-0.5), reads=[wres], writes=[wres])


def copy_on(K, e, out, in_, reads, writes, merge=()):
    K.op(e, I_acopy(out, in_) if e == "act" else I_copy(out, in_), reads=reads, writes=writes, merge=merge)


class WRing:
    def __init__(self, K, sc, n, nm):
        self.slots = []
        for i in range(n):
            t, r = sc.sbr(f"{nm}w{i}", [128, 16, 128], BF16)
            self.slots.append((t, r, f"{nm}w{i}"))
        self.i = 0
        self.n = n

    def next(self):
        s = self.slots[self.i]
        self.i = (self.i + 1) % self.n
        return s


def wview(W):
    return W.rearrange("(kc p) n -> p kc n", p=128)


def gemm_fm_gen(K, wr, groups, epilogue, bankfn=None, split=1, prime=False):
    bankfn = bankfn or K.bank
    ppg = max(len(g) for g in groups)
    ahead = max(1, wr.n // ppg - 1)
    pending = {}

    def issue(g):
        sl = []
        for (W, c0, KC, src, sres) in groups[g]:
            t, r, key = wr.next()
            K.dma("pool", key, t[:, 0:KC, :], wview(W)[:, :, c0:c0 + 128], writes=[r])
            sl.append((t, r))
        pending[g] = sl

    for g in range(min(ahead, len(groups))):
        issue(g)
    if prime:
        yield
    for g in range(len(groups)):
        if g + ahead < len(groups):
            issue(g + ahead)
        sl = pending.pop(g)
        for t4 in range(4):
            ps = []
            for (W, c0, KC, src, sres), (wt, wres) in zip(groups[g], sl):
                bank, bres = bankfn()
                mm = [(bank[:], wt[:, kc, :], src[:, kc, t4 * 512:(t4 + 1) * 512], kc == 0, kc == KC - 1)
                      for kc in range(KC)]
                step = (KC + split - 1) // split
                for si, k0 in enumerate(range(0, KC, step)):
                    K.op("pe", I_mm(mm[k0:k0 + step]), reads=[wres] + sres(t4),
                         writes=[bres] if si == 0 else [], merge=[] if si == 0 else [bres])
                    if split > 1:
                        yield
                ps.append((bank, bres))
            epilogue(g, t4, ps)
            yield


def gemm_fm(K, wr, groups, epilogue):
    for _ in gemm_fm_gen(K, wr, groups, epilogue):
        pass


def interleave(main, side, per):
    for _ in main:
        for _k in range(per):
            if side is not None:
                try:
                    next(side)
                except StopIteration:
                    side = None
    if side is not None:
        for _ in side:
            pass


def build(stage=99, debug=False):
    nc = bass.Bass("TRN2", target_bir_lowering=False)
    es = ExitStack()

    def din(name, shape, dt=F32):
        return nc.dram_tensor(name, shape, dt, kind="ExternalInput").ap()

    def dscr(name, shape, dt, dbg=False):
        if dbg and debug:
            return nc.dram_tensor(name, shape, dt, kind="ExternalOutput").ap()
        return nc.dram_tensor(name, shape, dt).ap()

    x_in = din("x", [S, D])
    p_in = din("p", [S, 256])
    out = nc.dram_tensor("out", [S, D], F32, kind="ExternalOutput").ap()
    W = {}
    for nm, shp in (("ffn1_w_gate", [D, DFF]), ("ffn1_w_up", [D, DFF]), ("ffn1_w_down", [DFF, D]),
                    ("w_in", [D, NIN]), ("w_branch_a", [1024, D]), ("w_branch_b", [1024, D]),
                    ("w_out", [D, D]), ("ffn2_w_gate", [D, DFF]), ("ffn2_w_up", [D, DFF]),
                    ("ffn2_w_down", [DFF, D]), ("w_ple_gate", [D, D]), ("w_ple_proj", [256, D])):
        W[nm] = din(nm, shp)
    V = {}
    for nm, n in (("ffn1_norm", D), ("mix_norm", D), ("ffn2_norm", D), ("ple_gate_norm", D),
                  ("ple_post_norm", D), ("q_norm", 64), ("k_norm", 64), ("lambda_q1", 64),
                  ("lambda_k1", 64), ("lambda_q2", 64), ("lambda_k2", 64), ("diff_subln", 128),
                  ("hgrn_norm", 128)):
        V[nm] = din(nm, [1, n])
    rel_bias = din("rel_bias", [32, 8])
    lb_logits = din("hgrn_lb_logits", [2, 1024])
    c_ident = din("c_ident", [128, 128], BF16)
    c_blk64 = din("c_blk64", [128, 128], BF16)
    c_mask2 = din("c_mask2", [128, 128], F32)
    c_scan = din("c_scan", [1, S], F32)
    c_oh = din("c_oh", [33, BVL], F32)

    x1 = dscr("x1", [S, D], F32, dbg=True) if stage > 1 else out
    x2 = dscr("x2", [S, D], F32, dbg=True) if stage > 2 else out
    x3 = dscr("x3", [S, D], F32, dbg=True) if stage > 3 else out
    gT = dscr("gT", [16, 128, 44, 128], BF16)
    mT = dscr("mT", [16, 128, 16, 128], BF16)
    yaT = dscr("yaT", [8, 128, S], BF16, dbg=True)
    ybT = dscr("ybT", [8, 128, S], BF16, dbg=True)
    BV = dscr("BV", [8, 128, BVL], F32)
    ple_d = dscr("ple", [S, D], F32)

    K = Sched(nc, es)
    top = Scope(K)
    es.enter_context(top.es)

    ident, ident_r = top.sbr("ident", [128, 128], BF16)
    K.dma("sp", "c_id", ident[:], c_ident, writes=[ident_r])

    with K.scope() as s0:
        rbx, rbx_r = s0.sbr("rbx", [33, 8], F32)
        K.op("dve", I_memset(rbx[:], -30000.0), writes=[rbx_r])
        K.dma("sp", "rbx", rbx[0:32, :], rel_bias, writes=[rbx_r])
        BV_res = [top.res() for _ in range(8)]
        with K.scope() as sbv:
            oh, oh_r = sbv.sbr("oh", [33, BVL], F32)
            K.dma("sp", "oh", oh[:], c_oh, writes=[oh_r])
            rbhs = [sbv.sbr("rbh", [33, 128], F32) for _ in range(2)]
            bvss = [sbv.sbr("bvs", [128, BVL], F32) for _ in range(2)]
            for h in range(8):
                rbh, rbh_r = rbhs[h % 2]
                bvs, bvs_r = bvss[h % 2]
                K.op("dve", I_copy(rbh[:], rbx[:, h:h + 1].to_broadcast([33, 128])), reads=[rbx_r],
                     writes=[rbh_r])
                for c3 in range(3):
                    bank, bres = K.bank()
                    K.op("pe", I_mm([(bank[:, 0:384], rbh[:], oh[:, c3 * 384:(c3 + 1) * 384], True, True)]),
                         reads=[rbh_r, oh_r], writes=[bres])
                    copy_on(K, "dve", bvs[:, c3 * 384:(c3 + 1) * 384], bank[:, 0:384], [bres],
                            [bvs_r] if c3 == 0 else [], merge=[] if c3 == 0 else [bvs_r])
                K.dma("sp", f"bvst{h % 2}", BV[h], bvs[:], reads=[bvs_r], writes=[BV_res[h]])


    def tok_rows(ap, tt):
        return ap[tt * 128:(tt + 1) * 128, :]

    def norm_T(src, src_res, gain, hT, hT_res, nm):
        with K.scope() as sc:
            gb, gb_r = sc.sbr("gb", [128, D], F32)
            xin = [sc.sbr("xin", [128, D], F32) for _ in range(3)]
            hn = [sc.sbr("hn", [128, D], BF16) for _ in range(3)]
            junk, junk_r = sc.sbr("junk", [128, D], BF16)
            ss = [sc.sbr("ss", [128, 1], F32) for _ in range(3)]
            rs = [sc.sbr("rs", [128, 1], F32) for _ in range(3)]
            K.dma("sp", nm + "gb", gb[:], gain[0].partition_broadcast(128), writes=[gb_r])
            K.op("dve", I_ts(gb[:], gb[:], math.sqrt(D), None, ALU.mult), reads=[gb_r], writes=[gb_r])
            def stageA(tt):
                b = tt % 3
                xt, xr = xin[b]
                K.dma("sp", f"{nm}xin{b}", xt[:], tok_rows(src, tt),
                      reads=[src_res[tt]] if src_res else [], writes=[xr])
                K.op("act", I_act(junk[:], xt[:], AF.Square, accum=ss[b][0][:]), reads=[xr],
                     writes=[junk_r, ss[b][1]])
                rsqrt_act(K, rs[b][0][:], ss[b][0][:], D * EPS, [ss[b][1]], rs[b][1])
                K.op("dve", I_stt(hn[b][0][:], xt[:], rs[b][0][:], gb[:], ALU.mult, ALU.mult),
                     reads=[xr, rs[b][1], gb_r], writes=[hn[b][1]])

            stageA(0)
            for tt in range(16):
                b = tt % 3
                if tt + 1 < 16:
                    stageA(tt + 1)
                for half in range(2):
                    bank, bres = K.bank()
                    pb = bank[:].bitcast(BF16)
                    K.op("pe", I_tr([(pb[:, j * 128:(j + 1) * 128],
                                      hn[b][0][:, (half * 8 + j) * 128:(half * 8 + j + 1) * 128])
                                     for j in range(8)], ident[:]),
                         reads=[hn[b][1], ident_r], writes=[bres])
                    copy_on(K, "act" if half == 0 else "dve",
                            hT[:, half * 8:(half + 1) * 8, tt * 128:(tt + 1) * 128],
                            pb.rearrange("p (j c) -> p j c", j=8), [bres], [], merge=[hT_res[tt]])

    def tm_loadw(Wap, KC, nm, t, rl, fb):
        kch = [(k0, min(KC, k0 + 11)) for k0 in range(0, KC, 11)]
        for ci, (k0, k1) in enumerate(kch):
            K.dma("pool", f"{nm}wd{fb % 2}_{ci}", t[:, k0:k1, :],
                  wview(Wap)[:, k0:k1, fb * 512:(fb + 1) * 512], writes=[rl[ci]])

    def gemm_tm(Wap, KC, lhs_get, pre, epilogue, nm, nfb=4, wd0=None):
        with K.scope() as sc:
            kch = [(k0, min(KC, k0 + 11)) for k0 in range(0, KC, 11)]
            wd = [wd0] if wd0 is not None else []
            while len(wd) < 2:
                t = sc.sb("wd", [128, KC, 512], BF16)
                wd.append((t, [sc.res() for _ in kch]))

            def loadw(fb):
                t, rl = wd[fb % 2]
                tm_loadw(Wap, KC, nm, t, rl, fb)

            seq = [(fb, tt) for fb in range(nfb) for tt in range(16)]
            if wd0 is None:
                loadw(0)
            for i in range(min(2, len(seq))):
                pre(sc, i, *seq[i])
            for i, (fb, tt) in enumerate(seq):
                if tt == 0 and fb + 1 < nfb:
                    loadw(fb + 1)
                if i + 2 < len(seq):
                    pre(sc, i + 2, *seq[i + 2])
                lt, lres = lhs_get(i, fb, tt)
                wt, wrl = wd[fb % 2]
                bank, bres = K.bank()
                mm = [(bank[:], lt[:, kc, :], wt[:, kc, :], kc == 0, kc == KC - 1) for kc in range(KC)]
                for ci, (k0, k1) in enumerate(kch):
                    K.op("pe", I_mm(mm[k0:k1]), reads=lres + [wrl[ci]], writes=[bres] if ci == 0 else [],
                         merge=[] if ci == 0 else [bres])
                epilogue(sc, i, fb, tt, bank, bres)

    class TileRing:
        def __init__(self, sc, n, shape, dt, nm):
            self.s = [sc.sbr(nm, shape, dt) for _ in range(n)]
            self.n = n
            self.nm = nm

        def __call__(self, i):
            t, r = self.s[i % self.n]
            return t, r, f"{self.nm}{i % self.n}"

    def ffn(xs, xs_res, xd, xd_res, gain, wg, wu, wdn, nm, hook=None):
        with K.scope() as sc:
            hT = sc.sb("hT", [128, 16, S], BF16)
            hT_res = [sc.res() for _ in range(16)]
            norm_T(xs, xs_res, gain, hT, hT_res, nm + "n")
            gT_res = [sc.res() for _ in range(16)]
            wd0 = (sc.sb("wd0", [128, 44, 512], BF16), [sc.res() for _ in range(4)])
            with K.scope() as sb:
                wr = WRing(K, sb, 8, nm)
                s32 = [sb.sbr("s32", [128, 512], F32) for _ in range(2)]
                grow = [sb.sbr("grow", [128, S], BF16) for _ in range(2)]
                groups = [[(wg, j * 128, 16, hT, lambda t4: hT_res[4 * t4:4 * t4 + 4]),
                           (wu, j * 128, 16, hT, lambda t4: hT_res[4 * t4:4 * t4 + 4])] for j in range(44)]
                cnt = [0]

                def epi(j, t4, ps):
                    (bg, rg), (bu, ru) = ps
                    b = cnt[0] % 2
                    cnt[0] += 1
                    st, sr = s32[b]
                    gt, gr = grow[j % 2]
                    if j == 34 and t4 == 0:
                        tm_loadw(wdn, 44, nm + "d", wd0[0], wd0[1], 0)
                    K.op("act", I_act(st[:], bg[:], AF.Silu), reads=[rg], writes=[sr])
                    K.op("dve", I_tt(gt[:, t4 * 512:(t4 + 1) * 512], st[:], bu[:], ALU.mult),
                         reads=[sr, ru], writes=[], merge=[gr])
                    if t4 == 3:
                        K.dma("sp", f"{nm}grow{j % 2}", gT[:, :, j, :].rearrange("t p c -> p t c"),
                              gt[:].rearrange("p (t c) -> p t c", c=128), reads=[gr], merge=gT_res)

                interleave(gemm_fm_gen(K, wr, groups, epi), hook(sb) if hook else None, 1)
            rings = {}

            def pre(sc2, i, fb, tt):
                if "g" not in rings:
                    rings["g"] = TileRing(sc2, 3, [128, 44, 128], BF16, nm + "gt")
                    rings["x"] = TileRing(sc2, 3, [128, 512], F32, nm + "xr")
                    rings["o"] = TileRing(sc2, 3, [128, 512], F32, nm + "xo")
                t, r, key = rings["g"](i)
                K.dma("sp", key, t[:], gT[tt], reads=[gT_res[tt]], writes=[r])
                t, r, key = rings["x"](i)
                K.dma("sp", key, t[:], xs[tt * 128:(tt + 1) * 128, fb * 512:(fb + 1) * 512],
                      reads=[xs_res[tt]] if xs_res else [], writes=[r])

            def lhs_get(i, fb, tt):
                t, r, _ = rings["g"](i)
                return t, [r]

            def epi2(sc2, i, fb, tt, bank, bres):
                xt, xr, _ = rings["x"](i)
                ot, orr, key = rings["o"](i)
                K.op("dve", I_stt(ot[:], bank[:], 0.5, xt[:], ALU.mult, ALU.add), reads=[bres, xr], writes=[orr])
                K.dma("sp", key, xd[tt * 128:(tt + 1) * 128, fb * 512:(fb + 1) * 512], ot[:],
                      reads=[orr], merge=[xd_res[tt]])

            gemm_tm(wdn, 44, lhs_get, pre, epi2, nm + "d", wd0=wd0)

    def fm_rownorm(src, src_r, gain, gain_r, mulrow, mul_r, dst, dst_r, sq, sq_r, rstd, rstd_r,
                   tmp, tmp_r, ones, ones_r, n):
        K.op("act", I_act(sq[:], src[:], AF.Square), reads=[src_r], writes=[sq_r])
        for t4 in range(4):
            sl = slice(t4 * 512, (t4 + 1) * 512)
            bank, bres = K.bank()
            K.op("pe", I_mm([(bank[:], ones[:], sq[:, sl], True, True)]), reads=[ones_r, sq_r], writes=[bres])
            K.op("act", I_act(rstd[:, sl], bank[:], AF.Ln, bias=float(n * EPS)), reads=[bres], writes=[],
                 merge=[rstd_r])
        K.op("act", I_act(rstd[:], rstd[:], AF.Exp, scale=-0.5), reads=[rstd_r], writes=[rstd_r])
        if mulrow is None:
            K.op("dve", I_stt(dst[:], src[:], gain, rstd[:], ALU.mult, ALU.mult),
                 reads=[src_r, gain_r, rstd_r], writes=[dst_r])
        else:
            K.op("dve", I_stt(tmp[:], src[:], gain, rstd[:], ALU.mult, ALU.mult),
                 reads=[src_r, gain_r, rstd_r], writes=[tmp_r])
            K.op("dve", I_tt(dst[:], tmp[:], mulrow[:], ALU.mult), reads=[tmp_r, mul_r], writes=[dst_r])

    def transpose_row(row, row_r, dst, dst_r, bankfn=None):
        for half in range(2):
            bank, bres = (bankfn or K.bank)()
            pb = bank[:].bitcast(BF16)
            K.op("pe", I_tr([(pb[:, j * 128:(j + 1) * 128],
                              row[:, (half * 8 + j) * 128:(half * 8 + j + 1) * 128]) for j in range(8)],
                            ident[:]), reads=[row_r, ident_r], writes=[bres])
            copy_on(K, "act" if half == 0 else "dve", dst[:, half * 8:(half + 1) * 8, :],
                    pb.rearrange("p (j c) -> p j c", j=8), [bres], [], merge=[dst_r])

    def colvec(sc, ap1n, n, nm, mul=None, reps=1):
        t, r = sc.sbr(nm, [n * reps, 1], F32)
        for i in range(reps):
            K.dma("sp", nm, t[i * n:(i + 1) * n, :], ap1n.rearrange("o n -> n o"), merge=[r])
        if mul is not None:
            K.op("dve", I_ts(t[:], t[:], float(mul), None, ALU.mult), reads=[r], writes=[r])
        return t, r

    def mixer(xs, xs_res, xd, xd_res):
        w_in = W["w_in"]
        mT_res = [top.res() for _ in range(16)]
        with K.scope() as sc:
            hT = sc.sb("h2T", [128, 16, S], BF16)
            hT_res = [sc.res() for _ in range(16)]
            norm_T(xs, xs_res, V["mix_norm"], hT, hT_res, "mn")
            hres4 = lambda t4: hT_res[4 * t4:4 * t4 + 4]
            ya_res = [sc.res() for _ in range(8)]
            yb_res = [sc.res() for _ in range(8)]
            ones, ones_r = sc.sbr("ones", [128, 128], BF16)
            K.op("dve", I_memset(ones[:], 1.0), writes=[ones_r])
            blk, blk_r = sc.sbr("blk", [128, 128], BF16)
            K.dma("sp", "c_blk", blk[:], c_blk64, writes=[blk_r])

            with K.scope() as sa:
                wr = WRing(K, sa, 6, "at")
                qg8, qg8_r = colvec(sa, V["q_norm"], 64, "qg8", 8.0, reps=2)
                kg8, kg8_r = colvec(sa, V["k_norm"], 64, "kg8", 8.0, reps=2)
                sg, sg_r = colvec(sa, V["diff_subln"], 128, "sg", math.sqrt(128.0) * (1.0 - LAMBDA_INIT))
                cball, cball_r = sa.sbr("cball", [128, 8], F32)
                K.dma("sp", "cball", cball[:], rel_bias[31].partition_broadcast(128), writes=[cball_r])
                lv = []
                for nm in ("lambda_q1", "lambda_k1", "lambda_q2", "lambda_k2"):
                    t, r = sa.sbr(nm, [128, 64], F32)
                    K.dma("sp", nm, t[:], V[nm][0].partition_broadcast(128), writes=[r])
                    lv.append((t, r))
                e12 = []
                for a, bb in ((0, 1), (2, 3)):
                    pr, pr_r = sa.sbr("lpr", [128, 64], F32)
                    sm, sm_r = sa.sbr("lsm", [128, 1], F32)
                    K.op("dve", I_tt(pr[:], lv[a][0][:], lv[bb][0][:], ALU.mult), reads=[lv[a][1], lv[bb][1]],
                         writes=[pr_r])
                    K.op("dve", (lambda pr=pr, sm=sm: lambda eng: eng.reduce_sum(out=sm[:], in_=pr[:], axis=AX.X))(),
                         reads=[pr_r], writes=[sm_r])
                    K.op("act", I_act(sm[:], sm[:], AF.Exp), reads=[sm_r], writes=[sm_r])
                    e12.append((sm, sm_r))
                neglam, neglam_r = sa.sbr("neglam", [128, 1], F32)
                K.op("dve", I_tt(neglam[:], e12[1][0][:], e12[0][0][:], ALU.subtract),
                     reads=[e12[0][1], e12[1][1]], writes=[neglam_r])
                K.op("dve", I_ts(neglam[:], neglam[:], -LAMBDA_INIT, None, ALU.add), reads=[neglam_r],
                     writes=[neglam_r])
                if SUB <= 1:
                    return
                q32, q32_r = sa.sbr("q32", [128, S], F32)
                rstd, rstd_r = sa.sbr("rstd", [128, S], F32)
                rstd2, rstd2_r = sa.sbr("rstd2", [128, S], F32)
                sq, sq_r = sa.sbr("sq", [128, S], BF16)
                yrow, yrow_r = sa.sbr("yrow", [128, S], BF16)
                vT, vT_r = sa.sbr("vT", [128, S], BF16)
                hb = [dict(qn=sa.sbr("qn", [128, S], BF16), kn=sa.sbr("kn", [128, S], BF16),
                           Vh=sa.sbr("Vh", [128, 16, 128], BF16)) for _ in range(2)]
                o32, o32_r = sa.sbr("o32", [128, S], F32)
                TTs = [sa.sbr("TT", [128, 1024], F32) for _ in range(2)]
                TT8s = [sa.sbr("TT8", [128, 1024], BF16) for _ in range(2)]
                zcol, zcol_r = sa.sbr("zcol", [128, 1], F32)
                K.op("dve", I_memset(zcol[:], 0.0), writes=[zcol_r])
                Ps = [sa.sbr("P", [128, 2, 512], BF16) for _ in range(4)]
                ep = [sa.sbr("ep", [128, 512], F32) for _ in range(4)]
                cn = {"s": 0, "p": 0, "sb": 0, "pb": 0}
                sq4 = [sa.res() for _ in range(4)]
                q324 = [sa.res() for _ in range(4)]
                rstd4 = [sa.res() for _ in range(4)]
                Laccs = [sa.sbr("Lacc", [128, 2, 512], F32) for _ in range(2)]
                LaccB, LaccB_r = sa.sbr("LaccB", [128, 2, 512], BF16)

                def proj_bank():
                    b = K.banks[6 + cn["pb"] % 2]
                    cn["pb"] += 1
                    return b

                def att_proj(h):
                    qn, qn_r = hb[h % 2]["qn"]
                    kn, kn_r = hb[h % 2]["kn"]
                    Vh, Vh_r = hb[h % 2]["Vh"]
                    TT, TT_r = TTs[h % 2]
                    K.dma("sp", f"TT{h % 2}", TT[:],
                          bass.AP(BV.tensor, h * 128 * BVL + 127, [[BVL - 1, 128], [1, 1024]]),
                          reads=[BV_res[h]], writes=[TT_r])
                    TT8, TT8_r = TT8s[h % 2]
                    K.op("dve", I_ts(TT8[:], TT[:], 8.0, None, ALU.mult), reads=[TT_r], writes=[TT8_r])
                    groups = [[(w_in, h * 128, 16, hT, hres4)], [(w_in, 1024 + h * 128, 16, hT, hres4)],
                              [(w_in, 2048 + h * 128, 16, hT, hres4)]]
                    pendB = []

                    def epi(g, t4, ps):
                        (bank, bres), = ps
                        sl = slice(t4 * 512, (t4 + 1) * 512)
                        if g == 2:
                            while pendB:
                                pendB.pop(0)()
                            copy_on(K, "act", vT[:, sl], bank[:], [bres], [], merge=[vT_r])
                            if t4 == 3:
                                transpose_row(vT, vT_r, Vh, Vh_r)
                            return
                        dst, dst_r, gn, gn_r = (qn, qn_r, qg8, qg8_r) if g == 0 else (kn, kn_r, kg8, kg8_r)
                        K.op("act", I_act(sq[:, sl], bank[:], AF.Square), reads=[bres], writes=[sq4[t4]])
                        K.op("dve", I_copy(q32[:, sl], bank[:]), reads=[bres], writes=[q324[t4]])

                        def stageB(sl=sl, dst=dst, dst_r=dst_r, gn=gn, gn_r=gn_r, t4=t4):
                            b2, b2r = K.bank()
                            K.op("pe", I_mm([(b2[:], blk[:], sq[:, sl], True, True)]), reads=[blk_r, sq4[t4]],
                                 writes=[b2r])
                            K.op("act", I_act(rstd[:, sl], b2[:], AF.Ln, bias=float(64 * EPS)), reads=[b2r],
                                 writes=[rstd4[t4]])
                            K.op("act", I_act(rstd[:, sl], rstd[:, sl], AF.Exp, scale=-0.5), reads=[rstd4[t4]],
                                 writes=[rstd4[t4]])
                            K.op("dve", I_stt(dst[:, sl], q32[:, sl], gn[:], rstd[:, sl], ALU.mult, ALU.mult),
                                 reads=[q324[t4], gn_r, rstd4[t4]], writes=[], merge=[dst_r])

                        while pendB:
                            pendB.pop(0)()
                        pendB.append(stageB)

                    yield from gemm_fm_gen(K, wr, groups, epi, prime=True)
                    while pendB:
                        pendB.pop(0)()
                    yield

                def bc2(ap, n):
                    return bass.AP(ap.tensor, ap.offset, [list(ap.ap[0]), [0, 2], list(ap.ap[1])])

                def att_loop(h):
                    qn, qn_r = hb[h % 2]["qn"]
                    kn, kn_r = hb[h % 2]["kn"]
                    Vh, Vh_r = hb[h % 2]["Vh"]
                    TT8, TT8_r = TT8s[h % 2]
                    acc = [K.banks[6], K.banks[0], K.banks[7], K.banks[1]]
                    for qt in range(4):
                        nkb = 4 * qt + 4
                        Lacc, Lacc_r = Laccs[qt % 2]
                        Sb = {}

                        def emit_S(kb, qt=qt):
                            c0 = max(0, 128 * (kb - 4 * qt))
                            p = cn["sb"] % 3
                            cn["sb"] += 1
                            near = (kb - 4 * qt) >= -1
                            o0 = 384 - 128 * (kb - 4 * qt)
                            for c in range(2):
                                bS, bSr = K.banks[2 * p + c]
                                mm = [(bS[:, c0:512], kn[c * 64:(c + 1) * 64, kb * 128:(kb + 1) * 128],
                                       qn[c * 64:(c + 1) * 64, qt * 512 + c0:(qt + 1) * 512], True, not near)]
                                if near:
                                    mm.append((bS[:, c0:512], ident[:], TT8[:, o0 + c0:o0 + 512], False, True))
                                K.op("pe", I_mm(mm), reads=[kn_r, qn_r] + ([ident_r, TT8_r] if near else []),
                                     writes=[bSr])
                            Sb[kb] = p

                        emit_S(0)
                        emit_S(1)
                        for kb in range(nkb):
                            if kb + 2 < nkb:
                                emit_S(kb + 2)
                            i = kb - 4 * qt
                            c0 = max(0, 128 * i)
                            p = Sb.pop(kb)
                            bres2 = [K.banks[2 * p][1], K.banks[2 * p + 1][1]]
                            S2 = K.ps[:, 2 * p:2 * p + 2, c0:512]
                            P, P_r = Ps[cn["p"] % 4]
                            cn["p"] += 1
                            if i <= -2:
                                K.op("act", I_act(P[:, :, c0:512], S2, AF.Exp, bias=cball[:, h:h + 1],
                                                  scale=0.125), reads=bres2 + [cball_r], writes=[P_r])
                            else:
                                K.op("act", I_act(P[:, :, c0:512], S2, AF.Exp, bias=zcol[:], scale=0.125),
                                     reads=bres2 + [zcol_r], writes=[P_r])
                            for c in range(2):
                                (bO, bOr) = acc[2 * c]
                                K.op("pe", I_mm([(bO[:, c0:512], Vh[:, kb, :], P[:, c, c0:512], kb == 0,
                                                  kb == nkb - 1)]),
                                     reads=[Vh_r, P_r], writes=[] if kb else [bOr], merge=[bOr] if kb else [])
                            if kb == 0:
                                K.op("dve", I_copy(Lacc[:], P[:]), reads=[P_r], writes=[Lacc_r])
                            else:
                                K.op("dve", I_tt(Lacc[:, :, c0:512], Lacc[:, :, c0:512], P[:, :, c0:512], ALU.add),
                                     reads=[P_r, Lacc_r], writes=[Lacc_r])
                            yield
                        K.op("dve", I_copy(LaccB[:], Lacc[:]), reads=[Lacc_r], writes=[LaccB_r])
                        for c in range(2):
                            bL, bLr = acc[2 * c + 1]
                            K.op("pe", I_mm([(bL[:], ones[:], LaccB[:, c, :], True, True)]), reads=[ones_r, LaccB_r],
                                 writes=[bLr])
                        (bO1, rO1), (bL1, rL1), (bO2, rO2), (bL2, rL2) = acc
                        (r1, r1r), (r2, r2r), (o1, o1r), (t2, t2r) = ep
                        sl = slice(qt * 512, (qt + 1) * 512)
                        K.op("act", I_act(r1[:], bL1[:], AF.Ln), reads=[rL1], writes=[r1r])
                        K.op("act", I_act(r2[:], bL2[:], AF.Ln), reads=[rL2], writes=[r2r])
                        K.op("act", I_act(r1[:], r1[:], AF.Exp, scale=-1.0), reads=[r1r], writes=[r1r])
                        K.op("act", I_act(r2[:], r2[:], AF.Exp, scale=-1.0), reads=[r2r], writes=[r2r])
                        K.op("dve", I_tt(o1[:], bO1[:], r1[:], ALU.mult), reads=[rO1, r1r], writes=[o1r])
                        K.op("dve", I_stt(t2[:], bO2[:], neglam[:], r2[:], ALU.mult, ALU.mult),
                             reads=[rO2, neglam_r, r2r], writes=[t2r])
                        K.op("dve", I_tt(o32[:, sl], o1[:], t2[:], ALU.add), reads=[o1r, t2r], writes=[],
                             merge=[o32_r])
                        yield
                    if SUBA <= 3:
                        return
                    fm_rownorm(o32, o32_r, sg[:], sg_r, None, None, yrow, yrow_r, yrow, yrow_r, rstd2, rstd2_r,
                               None, None, ones, ones_r, 128)
                    K.dma("sp", "yast", yaT[h], yrow[:], reads=[yrow_r], writes=[ya_res[h]])
                    yield

                gp = att_proj(0)
                next(gp)
                for h in range(NH):
                    for _ in gp:
                        pass
                    if h + 1 < NH:
                        gp = att_proj(h + 1)
                        next(gp)
                    for _ in att_loop(h):
                        pass

            if SUB <= 2:
                return
            with K.scope() as sh:
                wr = WRing(K, sh, 6, "hg")
                hgn, hgn_r = colvec(sh, V["hgrn_norm"], 128, "hgn", math.sqrt(128.0))
                lg, lg_r = sh.sbr("lg", [128, 2, 8], F32)
                for r_ in range(2):
                    for h_ in range(8):
                        K.dma("sp", "lg", lg[:, r_, h_:h_ + 1],
                              lb_logits[r_:r_ + 1, h_ * 128:(h_ + 1) * 128].rearrange("o n -> n o"), merge=[lg_r])
                lb, lb_r = sh.sbr("lb", [128, 8], F32)
                oml, oml_r = sh.sbr("oml", [128, 8], F32)
                K.op("dve", I_tt(lb[:], lg[:, 0, :], lg[:, 1, :], ALU.subtract), reads=[lg_r], writes=[lb_r])
                K.op("act", I_act(lb[:], lb[:], AF.Sigmoid), reads=[lb_r], writes=[lb_r])
                K.op("dve", I_ts(oml[:], lb[:], -1.0, 1.0, ALU.mult, ALU.add), reads=[lb_r], writes=[oml_r])
                scm, scm_r = sh.sbr("scm", [128, S], F32)
                K.dma("sp", "scm", scm[:], c_scan[0].partition_broadcast(128), writes=[scm_r])
                mk2, mk2_r = sh.sbr("mk2", [128, 128], F32)
                K.dma("sp", "mk2", mk2[:], c_mask2, writes=[mk2_r])
                hs = [dict(R1=sh.sbr("R1", [128, S], F32),
                           R2=sh.sbr("R2", [128, S], F32),
                           viT=sh.sbr("viT", [128, S], BF16),
                           og=sh.sbr("og", [128, S], BF16),
                           Vh=sh.sbr("Vhh", [128, 16, 128], BF16)) for _ in range(2)]
                R3, R3r = sh.sbr("R3", [128, S], F32)
                R4, R4r = sh.sbr("R4", [128, S], F32)
                R5, R5r = sh.sbr("R5", [128, S], F32)
                Qt, Qt_r = sh.sbr("Qt", [128, S], BF16)
                Kt, Kt_r = sh.sbr("Kt", [128, S], BF16)
                Kh, Kh_r = sh.sbr("Kh", [128, S], BF16)
                KhT, KhT_r = sh.sbr("KhT", [128, 16, 128], BF16)
                st32 = [sh.sbr("st32", [128, 128], F32) for _ in range(2)]
                st16 = [sh.sbr("st16", [128, 128], BF16) for _ in range(6)]
                STm = [sh.sbr("STm", [128, 128], BF16) for _ in range(2)]

                def hg_proj(h):
                    st = hs[h % 2]
                    (R1, R1r), (R2, R2r), (viT, viT_r), (og, og_r), (Vh, Vh_r) = (st["R1"], st["R2"], st["viT"],
                                                                                   st["og"], st["Vh"])
                    order = [2, 0, 1, 3]
                    groups = [[(w_in, (3 + g) * 1024 + h * 128, 16, hT, hres4)] for g in order]

                    def epi(gi, t4, ps):
                        g = order[gi]
                        (bank, bres), = ps
                        sl = slice(t4 * 512, (t4 + 1) * 512)
                        if g == 0:
                            K.op("act", I_act(R1[:, sl], bank[:], AF.Silu), reads=[bres], writes=[], merge=[R1r])
                        elif g == 1:
                            K.op("act", I_act(R2[:, sl], bank[:], AF.Sigmoid), reads=[bres], writes=[], merge=[R2r])
                        elif g == 2:
                            copy_on(K, "dve", viT[:, sl], bank[:], [bres], [], merge=[viT_r])
                            if t4 == 3:
                                transpose_row(viT, viT_r, Vh, Vh_r)
                        else:
                            K.op("act", I_act(og[:, sl], bank[:], AF.Silu), reads=[bres], writes=[], merge=[og_r])

                    yield from gemm_fm_gen(K, wr, groups, epi, split=2)

                def hg_main(h):
                    st = hs[h % 2]
                    (R1, R1r), (R2, R2r), (viT, viT_r), (og, og_r), (Vh, Vh_r) = (st["R1"], st["R2"], st["viT"],
                                                                                   st["og"], st["Vh"])
                    K.op("dve", I_ts(R2[:], R2[:], oml[:, h:h + 1], lb[:, h:h + 1], ALU.mult, ALU.add),
                         reads=[R2r, oml_r, lb_r], writes=[R2r])
                    yield
                    K.op("act", I_act(R3[:], R2[:], AF.Ln), reads=[R2r], writes=[R3r])
                    K.op("dve", I_ts(R2[:], R2[:], -1.0, 1.0, ALU.mult, ALU.add), reads=[R2r], writes=[R2r])
                    yield
                    K.op("dve", lambda eng: eng.tensor_tensor_scan(out=R4[:], data0=scm[:], data1=R3[:], initial=0.0,
                                                                   op0=ALU.mult, op1=ALU.add),
                         reads=[scm_r, R3r], writes=[R4r])
                    K.op("act", I_act(R5[:], R4[:], AF.Exp), reads=[R4r], writes=[R5r])
                    yield
                    K.op("act", I_act(R3[:], R4[:], AF.Exp, scale=-1.0), reads=[R4r], writes=[R3r])
                    K.op("dve", I_tt(Qt[:], R1[:], R5[:], ALU.mult), reads=[R1r, R5r], writes=[Qt_r])
                    yield
                    K.op("dve", I_tt(Kt[:], R2[:], R3[:], ALU.mult), reads=[R2r, R3r], writes=[Kt_r])
                    b3 = R4[:].rearrange("p (c t) -> p c t", t=64)
                    K.op("dve", I_tt(R3[:].rearrange("p (c t) -> p c t", t=64),
                                     b3[:, :, 63:64].to_broadcast([128, 32, 64]), b3, ALU.subtract),
                         reads=[R4r], writes=[R3r])
                    yield
                    K.op("act", I_act(R3[:], R3[:], AF.Exp), reads=[R3r], writes=[R3r])
                    K.op("dve", I_tt(Kh[:], R2[:], R3[:], ALU.mult), reads=[R2r, R3r], writes=[Kh_r])
                    transpose_row(Kh, Kh_r, KhT, KhT_r)
                    yield
                    e3 = R5[:].rearrange("p (c t) -> p c t", t=64)
                    K.op("dve", I_memset(st32[0][0][:], 0.0), writes=[st32[0][1]])
                    K.op("dve", I_memset(st16[0][0][:], 0.0), writes=[st16[0][1]])
                    pre_ = {}

                    def emit_pre(j):
                        tk = slice(j * 128, (j + 1) * 128)
                        bST, bSTr = K.bank()
                        K.op("pe", I_mm([(bST[:, 0:128], Kt[:, tk], Qt[:, tk], True, True)]), reads=[Kt_r, Qt_r],
                             writes=[bSTr])
                        bU = [K.bank(), K.bank()]
                        for u in range(2):
                            K.op("pe", I_mm([(bU[u][0][:, 0:128], KhT[u * 64:(u + 1) * 64, j, :],
                                              Vh[u * 64:(u + 1) * 64, j, :], True, True)]),
                                 reads=[KhT_r, Vh_r], writes=[bU[u][1]])
                        pre_[j] = (bST, bSTr, bU)

                    emit_pre(0)
                    for j in range(16):
                        tk = slice(j * 128, (j + 1) * 128)
                        bST, bSTr, bU = pre_.pop(j)
                        sm, sm_r = STm[j % 2]
                        K.op("dve", I_tt(sm[:], bST[:, 0:128], mk2[:], ALU.mult), reads=[bSTr, mk2_r], writes=[sm_r])
                        s16 = [st16[(2 * j) % 6], st16[(2 * j + 1) % 6], st16[(2 * j + 2) % 6]]
                        s32 = [st32[0], st32[1], st32[0]]
                        for u in range(2):
                            K.op("dve", I_stt(s32[u + 1][0][:], s32[u][0][:], e3[:, 2 * j + u, 63:64], bU[u][0][:, 0:128],
                                              ALU.mult, ALU.add), reads=[s32[u][1], R5r, bU[u][1]], writes=[s32[u + 1][1]])
                            copy_on(K, "act", s16[u + 1][0][:], s32[u + 1][0][:], [s32[u + 1][1]], [s16[u + 1][1]])
                        if j + 1 < 16:
                            emit_pre(j + 1)
                        bO, bOr = K.bank()
                        K.op("pe", I_mm([(bO[:, 0:128], Vh[:, j, :], sm[:], True, False),
                                         (bO[:, 0:64], s16[0][0][:], Qt[:, j * 128:j * 128 + 64], False, False),
                                         (bO[:, 64:128], s16[1][0][:], Qt[:, j * 128 + 64:(j + 1) * 128], False, True)]),
                             reads=[Vh_r, sm_r, s16[0][1], s16[1][1], Qt_r], writes=[bOr])
                        copy_on(K, "act", R1[:, tk], bO[:, 0:128], [bOr], [], merge=[R1r])
                        yield
                    fm_rownorm(R1, R1r, hgn[:], hgn_r, og, og_r, Kh, Kh_r, viT, viT_r, R4, R4r, R3, R3r,
                               ones, ones_r, 128)
                    K.dma("sp", "ybst", ybT[h], Kh[:], reads=[Kh_r], writes=[yb_res[h]])
                    yield

                for _ in hg_proj(0):
                    pass
                for h in range(8):
                    interleave(hg_main(h), hg_proj(h + 1) if h + 1 < 8 else None, 2)

            if SUB <= 3:
                return
            with K.scope() as sm_:
                wr = WRing(K, sm_, 8, "mg")
                yas = sm_.sb("yas", [128, 8, S], BF16)
                ybs = sm_.sb("ybs", [128, 8, S], BF16)
                yas_r = [sm_.res() for _ in range(8)]
                ybs_r = [sm_.res() for _ in range(8)]
                for h in range(8):
                    K.dma("sp", "yald", yas[:, h, :], yaT[h], reads=[ya_res[h]], writes=[yas_r[h]])
                    K.dma("sp", "ybld", ybs[:, h, :], ybT[h], reads=[yb_res[h]], writes=[ybs_r[h]])
                sg_ = [sm_.sbr("sga", [128, 512], F32) for _ in range(2)]
                sb_ = [sm_.sbr("sgb", [128, 512], F32) for _ in range(2)]
                mrow = [sm_.sbr("mrow", [128, S], BF16) for _ in range(2)]
                groups = [[(w_in, 7168 + m * 128, 16, hT, hres4), (w_in, 9216 + m * 128, 16, hT, hres4),
                           (W["w_branch_a"], m * 128, 8, yas, lambda t4: yas_r),
                           (W["w_branch_b"], m * 128, 8, ybs, lambda t4: ybs_r)] for m in range(16)]
                cnt = [0]

                def epi(m, t4, ps):
                    (bga, rga), (bgb, rgb), (bba, rba), (bbb, rbb) = ps
                    b = cnt[0] % 2
                    cnt[0] += 1
                    (ta, tar), (tb, tbr) = sg_[b], sb_[b]
                    mt, mr = mrow[m % 2]
                    sl = slice(t4 * 512, (t4 + 1) * 512)
                    K.op("act", I_act(ta[:], bga[:], AF.Sigmoid), reads=[rga], writes=[tar])
                    K.op("act", I_act(tb[:], bgb[:], AF.Sigmoid), reads=[rgb], writes=[tbr])
                    K.op("dve", I_tt(ta[:], ta[:], bba[:], ALU.mult), reads=[tar, rba], writes=[tar])
                    K.op("dve", I_tt(tb[:], tb[:], bbb[:], ALU.mult), reads=[tbr, rbb], writes=[tbr])
                    K.op("dve", I_tt(mt[:, sl], ta[:], tb[:], ALU.add), reads=[tar, tbr], writes=[], merge=[mr])
                    if t4 == 3:
                        K.dma("sp", f"mrow{m % 2}", mT[:, :, m, :].rearrange("t p c -> p t c"),
                              mt[:].rearrange("p (t c) -> p t c", c=128), reads=[mr], merge=mT_res)

                gemm_fm(K, wr, groups, epi)

        if SUB <= 4:
            return
        rings = {}

        def pre(sc2, i, fb, tt):
            if "g" not in rings:
                rings["g"] = [sc2.sbr("wo_m", [128, 16, 128], BF16) for _ in range(16)]
                for t_ in range(16):
                    K.dma("sp", f"wo_m{t_ % 4}", rings["g"][t_][0][:], mT[t_], reads=[mT_res[t_]],
                          writes=[rings["g"][t_][1]])
                rings["x"] = TileRing(sc2, 3, [128, 512], F32, "wo_xr")
                rings["o"] = TileRing(sc2, 3, [128, 512], F32, "wo_xo")
            t, r, key = rings["x"](i)
            K.dma("sp", key, t[:], xs[tt * 128:(tt + 1) * 128, fb * 512:(fb + 1) * 512],
                  reads=[xs_res[tt]], writes=[r])

        def lhs_get(i, fb, tt):
            t, r = rings["g"][tt]
            return t, [r]

        def epi2(sc2, i, fb, tt, bank, bres):
            xt, xr, _ = rings["x"](i)
            ot, orr, key = rings["o"](i)
            K.op("dve", I_tt(ot[:], bank[:], xt[:], ALU.add), reads=[bres, xr], writes=[orr])
            K.dma("sp", key, xd[tt * 128:(tt + 1) * 128, fb * 512:(fb + 1) * 512], ot[:],
                  reads=[orr], merge=[xd_res[tt]])

        gemm_tm(W["w_out"], 16, lhs_get, pre, epi2, "wo")

    def ple_proj_gen(sc, ple_res):
        if True:
            wp, wp_r = sc.sbr("wp", [128, 2, D], BF16)
            K.dma("pool", "wp", wp[:], wview(W["w_ple_proj"]), writes=[wp_r])
            gb, gb_r = sc.sbr("pgb", [128, D], F32)
            K.dma("sp", "pgb", gb[:], V["ple_post_norm"][0].partition_broadcast(128), writes=[gb_r])
            K.op("dve", I_ts(gb[:], gb[:], math.sqrt(D), None, ALU.mult), reads=[gb_r], writes=[gb_r])
            pin = [sc.sbr("pin", [128, 256], F32) for _ in range(2)]
            pbf = [sc.sbr("pbf", [128, 256], BF16) for _ in range(2)]
            pT = [sc.sbr("pT", [128, 2, 128], BF16) for _ in range(2)]
            pl32 = [sc.sbr("pl32", [128, D], F32) for _ in range(2)]
            junk, junk_r = sc.sbr("pjunk", [128, D], BF16)
            ss = [sc.sbr("pss", [128, 1], F32) for _ in range(2)]
            for tt in range(16):
                b = tt % 2
                K.dma("sp", f"pin{b}", pin[b][0][:], p_in[tt * 128:(tt + 1) * 128, :], writes=[pin[b][1]])
                K.op("dve", I_copy(pbf[b][0][:], pin[b][0][:]), reads=[pin[b][1]], writes=[pbf[b][1]])
                bank, bres = K.bank()
                pb = bank[:].bitcast(BF16)
                K.op("pe", I_tr([(pb[:, j * 128:(j + 1) * 128], pbf[b][0][:, j * 128:(j + 1) * 128])
                                 for j in range(2)], ident[:]), reads=[pbf[b][1], ident_r], writes=[bres])
                copy_on(K, "dve", pT[b][0][:], pb[:, 0:256].rearrange("p (j c) -> p j c", j=2), [bres], [pT[b][1]])
                for fb in range(4):
                    bank, bres = K.bank()
                    K.op("pe", I_mm([(bank[:], pT[b][0][:, kc, :], wp[:, kc, fb * 512:(fb + 1) * 512], kc == 0, kc == 1)
                                     for kc in range(2)]), reads=[pT[b][1], wp_r], writes=[bres])
                    copy_on(K, "act", pl32[b][0][:, fb * 512:(fb + 1) * 512], bank[:], [bres], [], merge=[pl32[b][1]])
                K.op("act", I_act(junk[:], pl32[b][0][:], AF.Square, accum=ss[b][0][:]), reads=[pl32[b][1]],
                     writes=[junk_r, ss[b][1]])
                rsqrt_act(K, ss[b][0][:], ss[b][0][:], D * EPS, [ss[b][1]], ss[b][1])
                K.op("dve", I_stt(pl32[b][0][:], pl32[b][0][:], ss[b][0][:], gb[:], ALU.mult, ALU.mult),
                     reads=[pl32[b][1], ss[b][1], gb_r], writes=[pl32[b][1]])
                K.dma("sp", f"plst{b}", ple_d[tt * 128:(tt + 1) * 128, :], pl32[b][0][:], reads=[pl32[b][1]],
                      writes=[ple_res[tt]])
                yield

    def ple_phase(xs, xs_res, xd, xd_res, ple_res):
        with K.scope() as sc:
            hT = sc.sb("h4T", [128, 16, S], BF16)
            hT_res = [sc.res() for _ in range(16)]
            norm_T(xs, xs_res, V["ple_gate_norm"], hT, hT_res, "pn")
            rings = {}

            def pre(sc2, i, fb, tt):
                if "x" not in rings:
                    rings["x"] = TileRing(sc2, 3, [128, 512], F32, "pg_xr")
                    rings["p"] = TileRing(sc2, 3, [128, 512], F32, "pg_pl")
                    rings["s"] = TileRing(sc2, 3, [128, 512], F32, "pg_sg")
                t, r, key = rings["x"](i)
                K.dma("sp", key, t[:], xs[tt * 128:(tt + 1) * 128, fb * 512:(fb + 1) * 512],
                      reads=[xs_res[tt]], writes=[r])
                t, r, key = rings["p"](i)
                K.dma("sp", key, t[:], ple_d[tt * 128:(tt + 1) * 128, fb * 512:(fb + 1) * 512],
                      reads=[ple_res[tt]], writes=[r])

            def lhs_get(i, fb, tt):
                return hT[:, :, tt * 128:(tt + 1) * 128], [hT_res[tt]]

            def epi2(sc2, i, fb, tt, bank, bres):
                xt, xr, _ = rings["x"](i)
                pt, pr, _ = rings["p"](i)
                st, sr, key = rings["s"](i)
                K.op("act", I_act(st[:], bank[:], AF.Sigmoid), reads=[bres], writes=[sr])
                K.op("dve", I_tt(st[:], st[:], pt[:], ALU.mult), reads=[sr, pr], writes=[sr])
                K.op("dve", I_tt(st[:], st[:], xt[:], ALU.add), reads=[sr, xr], writes=[sr])
                K.dma("sp", key, xd[tt * 128:(tt + 1) * 128, fb * 512:(fb + 1) * 512], st[:],
                      reads=[sr], merge=[xd_res[tt]])

            gemm_tm(W["w_ple_gate"], 16, lhs_get, pre, epi2, "pg")

    x1_res = [top.res() for _ in range(16)]
    x2_res = [top.res() for _ in range(16)]
    x3_res = [top.res() for _ in range(16)]
    out_res = [top.res() for _ in range(16)]

    ffn(x_in, None, x1, x1_res if stage > 1 else out_res, V["ffn1_norm"],
        W["ffn1_w_gate"], W["ffn1_w_up"], W["ffn1_w_down"], "f1")
    if stage > 1:
        mixer(x1, x1_res, x2, x2_res if stage > 2 else out_res)
    ple_res = [top.res() for _ in range(16)]

    def spread(gen, k):
        for _ in gen:
            for _k in range(k):
                yield

    if stage > 2:
        ffn(x2, x2_res, x3, x3_res if stage > 3 else out_res, V["ffn2_norm"],
            W["ffn2_w_gate"], W["ffn2_w_up"], W["ffn2_w_down"], "f2",
            hook=(lambda sc: spread(ple_proj_gen(sc, ple_res), 8)) if stage > 3 else None)
    if stage > 3:
        ple_phase(x3, x3_res, out, out_res, ple_res)

    K.wait_all("sp", out_res)
    K.emit()
    es.close()
    return nc


def _consts():
    bf = ml_dtypes.bfloat16
    ident = np.eye(128, dtype=np.float32).astype(bf)
    blk = np.zeros((128, 128), np.float32)
    blk[:64, :64] = 1
    blk[64:, 64:] = 1
    s = np.arange(128)[:, None]
    t = np.arange(128)[None, :]
    mask2 = ((s // 64 == t // 64) & (t >= s)).astype(np.float32)
    scan = np.ones((1, S), np.float32)
    scan[0, ::64] = 0
    oh = np.zeros((33, BVL), np.float32)
    for idx in range(BVL - 1):
        dist = idx - 511
        if dist < 0:
            oh[32, idx] = 1
        else:
            n = dist
            if n < 16:
                b = n
            else:
                nf = np.float32(max(n, 1))
                b = 16 + int(np.float32(np.log(nf / np.float32(16)) / np.float32(math.log(8.0))
                                        * np.float32(16)))
                b = min(b, 31)
            oh[b, idx] = 1
    return dict(c_ident=ident, c_blk64=blk.astype(bf), c_mask2=mask2, c_scan=scan, c_oh=oh)


_W2 = ("ffn1_w_gate", "ffn1_w_up", "ffn1_w_down", "w_in", "w_branch_a", "w_branch_b", "w_out",
       "ffn2_w_gate", "ffn2_w_up", "ffn2_w_down", "w_ple_gate", "w_ple_proj")
_V1 = ("ffn1_norm", "mix_norm", "ffn2_norm", "ple_gate_norm", "ple_post_norm", "q_norm", "k_norm",
       "lambda_q1", "lambda_k1", "lambda_q2", "lambda_k2", "diff_subln", "hgrn_norm")


def kernel(**inputs):
    stage = int(os.environ.get("MK_STAGE", "99"))
    debug = bool(int(os.environ.get("MK_DEBUG", "0")))
    nc = build(stage, debug)
    shared = {}
    for k in _W2:
        shared[k] = np.ascontiguousarray(np.asarray(inputs[k], np.float32)[0])
    for k in _V1:
        shared[k] = np.ascontiguousarray(np.asarray(inputs[k], np.float32))
    shared["rel_bias"] = np.ascontiguousarray(np.asarray(inputs["rel_bias"], np.float32))
    shared["hgrn_lb_logits"] = np.ascontiguousarray(np.asarray(inputs["hgrn_lb_logits"], np.float32))
    shared.update(_consts())
    x = np.asarray(inputs["x"], np.float32)
    p = np.asarray(inputs["p"], np.float32)
    in_maps = []
    ncores = int(os.environ.get("MK_CORES", NCORES))
    for c in range(ncores):
        m = dict(shared)
        m["x"] = np.ascontiguousarray(x[c])
        m["p"] = np.ascontiguousarray(p[0, c])
        in_maps.append(m)
    res = run_bass_kernel_spmd(nc, in_maps, core_ids=list(range(ncores)))
    if debug:
        kernel.last = res.results
    return np.stack([np.asarray(r["out"], np.float32) for r in res.results], axis=0)
```
